# Optimizing a Trainium2 kernel written in Bass

```python
import jax, jax.numpy as jnp
from jax import lax
import numpy as np


D_MODEL = 1024
BATCH = 16
SEQ = 4096
DEPTH = 1

HEAD_DIM = 64
NSA_HEADS = 8
NSA_KV_HEADS = 2
SWA_HEADS = 8
SWA_KV_HEADS = 2
N_HEADS = NSA_HEADS + SWA_HEADS
MIX_WIDTH = N_HEADS * HEAD_DIM
CMP_LEN = 32
CMP_STRIDE = 16
CMP_HIDDEN = 128
SEL_LEN = 64
N_SEL = 16
NSA_WINDOW = 512
NSA_Q_BLOCK = 64
SWA_WINDOW = 128
SWA_Q_BLOCK = 128
D_FF = 2816
CONV_W = 3
PLE_DIM = 256
RMS_EPS = 1e-6
NEG_INF = -1e30
FORCE_BONUS = 1e6

NSA_Q_W = NSA_HEADS * HEAD_DIM
NSA_KV_W = NSA_KV_HEADS * HEAD_DIM
NSA_GATE_W = 3 * NSA_HEADS
SWA_Q_W = SWA_HEADS * HEAD_DIM
SWA_KV_W = SWA_KV_HEADS * HEAD_DIM
IN_SPLITS = [NSA_Q_W] + [NSA_KV_W] * 6 + [NSA_GATE_W, SWA_Q_W, SWA_KV_W, SWA_KV_W]
IN_WIDTH = sum(IN_SPLITS)
IN_SPLIT_POINTS = [int(v) for v in np.cumsum(IN_SPLITS)[:-1]]

kernel_name = 'hybrid_nsa_swa_sink_convffn_block'


def rmsnorm(x, g):
    xf = x.astype(jnp.float32)
    y = xf * lax.rsqrt(jnp.mean(xf * xf, axis=-1, keepdims=True) + RMS_EPS)
    return (y * g.astype(jnp.float32)).astype(x.dtype)


def alibi_slopes():
    h = jnp.arange(N_HEADS, dtype=jnp.float32)
    return jnp.exp2(-8.0 * (h + 1.0) / N_HEADS)


def compress_kv(kv, pe, w1, w2):
    B, S, G, hd = kv.shape
    r = CMP_LEN // CMP_STRIDE
    seg = kv.reshape(B, S // CMP_STRIDE, CMP_STRIDE, G, hd)
    n_cmp = S // CMP_STRIDE - r + 1
    blocks = jnp.concatenate([seg[:, i:i + n_cmp] for i in range(r)], axis=2)
    blocks = blocks + pe.astype(kv.dtype)[:, None, :]
    flat = jnp.transpose(blocks, (0, 1, 3, 2, 4)).reshape(B, n_cmp, G, CMP_LEN * hd)
    hid = jax.nn.gelu(flat @ w1.astype(kv.dtype))
    return hid @ w2.astype(kv.dtype)


def nsa_attention(q, k_cmp, v_cmp, k_slc, v_slc, k_win, v_win, gates, slopes):
    B, S, H, hd = q.shape
    G = k_slc.shape[2]
    R = H // G
    C = NSA_Q_BLOCK
    n_cmp = k_cmp.shape[1]
    n_sb = S // SEL_LEN
    n_top = min(N_SEL, n_sb)
    scale = hd ** -0.5
    slopes_g = slopes.reshape(G, R)
    cmp_start = jnp.arange(n_cmp) * CMP_STRIDE
    cmp_end = cmp_start + CMP_LEN - 1
    sel_start = jnp.arange(n_sb) * SEL_LEN
    overlap = ((cmp_start[:, None] < sel_start[None, :] + SEL_LEN)
               & (cmp_end[:, None] >= sel_start[None, :])).astype(jnp.float32)
    kb = jnp.transpose(k_slc.reshape(B, n_sb, SEL_LEN, G, hd), (0, 3, 1, 2, 4))
    vb = jnp.transpose(v_slc.reshape(B, n_sb, SEL_LEN, G, hd), (0, 3, 1, 2, 4))
    pad = ((0, 0), (NSA_WINDOW, 0), (0, 0), (0, 0))
    k_wp = jnp.pad(k_win, pad)
    v_wp = jnp.pad(v_win, pad)
    bi = jnp.arange(B)[:, None, None, None]
    gi = jnp.arange(G)[None, :, None, None]
    blk = jnp.arange(n_sb)

    def one_block(c):
        t0 = c * C
        t = t0 + jnp.arange(C)
        qc = lax.dynamic_slice_in_dim(q, t0, C, axis=1).reshape(B, C, G, R, hd)
        gc = lax.dynamic_slice_in_dim(gates, t0, C, axis=1).reshape(B, C, G, R, 3)
        dist_c = (t[:, None] - cmp_end[None, :]).astype(jnp.float32)
        valid_c = dist_c >= 0
        s_c = jnp.einsum('bcgrd,bngd->bgrcn', qc, k_cmp).astype(jnp.float32) * scale
        s_c = jnp.where(valid_c, s_c - slopes_g[:, :, None, None] * dist_c, NEG_INF)
        p_c = jax.nn.softmax(s_c, axis=-1) * valid_c
        o_c = jnp.einsum('bgrcn,bngd->bcgrd', p_c.astype(v_cmp.dtype), v_cmp)
        imp = jnp.einsum('bgrcn,nj->bgcj', p_c, overlap)
        jt = t // SEL_LEN
        forced = (blk[None, :] == 0) | (blk[None, :] == jt[:, None]) | (blk[None, :] == jt[:, None] - 1)
        imp = jnp.where(forced, FORCE_BONUS, imp)
        imp = jnp.where(blk[None, :] <= jt[:, None], imp, NEG_INF)
        _, idx = lax.top_k(imp, n_top)
        ks = kb[bi, gi, idx]
        vs = vb[bi, gi, idx]
        pos = idx[..., None] * SEL_LEN + jnp.arange(SEL_LEN)
        dist_s = (t[:, None, None] - pos).astype(jnp.float32)[:, :, None]
        s_s = jnp.einsum('bcgrd,bgcnld->bgrcnl', qc, ks).astype(jnp.float32) * scale
        s_s = jnp.where(dist_s >= 0, s_s - slopes_g[None, :, :, None, None, None] * dist_s, NEG_INF)
        p_s = jax.nn.softmax(s_s, axis=(-2, -1))
        o_s = jnp.einsum('bgrcnl,bgcnld->bcgrd', p_s.astype(vs.dtype), vs)
        kw = lax.dynamic_slice_in_dim(k_wp, t0, C + NSA_WINDOW, axis=1)
        vw = lax.dynamic_slice_in_dim(v_wp, t0, C + NSA_WINDOW, axis=1)
        s_pos = t0 - NSA_WINDOW + jnp.arange(C + NSA_WINDOW)
        dist_w = t[:, None] - s_pos[None, :]
        valid_w = (dist_w >= 0) & (dist_w < NSA_WINDOW) & (s_pos[None, :] >= 0)
        s_w = jnp.einsum('bcgrd,bkgd->bgrck', qc, kw).astype(jnp.float32) * scale
        s_w = jnp.where(valid_w, s_w - slopes_g[:, :, None, None] * dist_w.astype(jnp.float32), NEG_INF)
        p_w = jax.nn.softmax(s_w, axis=-1)
        o_w = jnp.einsum('bgrck,bkgd->bcgrd', p_w.astype(vw.dtype), vw)
        o = gc[..., 0:1] * o_c + gc[..., 1:2] * o_s + gc[..., 2:3] * o_w
        return o.reshape(B, C, H * hd)

    out = lax.map(one_block, jnp.arange(S // C))
    return jnp.transpose(out, (1, 0, 2, 3)).reshape(B, S, H * hd)


def swa_attention(q, k, v, sinks, slopes):
    B, S, H, hd = q.shape
    G = k.shape[2]
    R = H // G
    C = SWA_Q_BLOCK
    n_q = S // C
    scale = hd ** -0.5
    qb = q.reshape(B, n_q, C, G, R, hd)
    pad = ((0, 0), (C, 0), (0, 0), (0, 0))
    kp = jnp.pad(k, pad).reshape(B, n_q + 1, C, G, hd)
    vp = jnp.pad(v, pad).reshape(B, n_q + 1, C, G, hd)
    kb = jnp.concatenate([kp[:, :-1], kp[:, 1:]], axis=2)
    vb = jnp.concatenate([vp[:, :-1], vp[:, 1:]], axis=2)
    i = jnp.arange(C)[:, None]
    j = jnp.arange(2 * C)[None, :]
    dist = C + i - j
    kpos = jnp.arange(n_q)[:, None, None] * C - C + j[None]
    valid = (dist >= 0) & (dist < SWA_WINDOW) & (kpos >= 0)
    slopes_g = slopes.reshape(G, R)
    s = jnp.einsum('bnqgrd,bnkgd->bngrqk', qb, kb).astype(jnp.float32) * scale
    s = s - slopes_g[:, :, None, None] * dist.astype(jnp.float32)
    s = jnp.where(valid[:, None, None], s, NEG_INF)
    sg = sinks.astype(jnp.float32).reshape(G, R)[:, :, None, None]
    m = jnp.maximum(jnp.max(s, axis=-1, keepdims=True), sg)
    e = jnp.exp(s - m)
    prob = e / (jnp.sum(e, axis=-1, keepdims=True) + jnp.exp(sg - m))
    o = jnp.einsum('bngrqk,bnkgd->bnqgrd', prob.astype(vb.dtype), vb)
    return o.reshape(B, S, H * hd)


def causal_dwconv(a, w, b):
    y = lax.conv_general_dilated(a, w[:, None, :].astype(a.dtype), window_strides=(1,),
                                 padding=[(CONV_W - 1, 0)],
                                 dimension_numbers=('NWC', 'WIO', 'NWC'),
                                 feature_group_count=a.shape[-1])
    return y + b.astype(a.dtype)


def setup_inputs(seed: int = 0) -> dict:
    key = jax.random.key(seed)
    ks = jax.random.split(key, 24)
    f32 = jnp.float32
    L = DEPTH

    def nrm(k, shape, scale):
        return jax.random.normal(k, shape, f32) * scale

    def gain(k):
        return 1.0 + 0.02 * jax.random.normal(k, (L, D_MODEL), f32)

    return {
        'x': nrm(ks[0], (BATCH, SEQ, D_MODEL), 1.0),
        'p': nrm(ks[1], (DEPTH, BATCH, SEQ, PLE_DIM), 1.0),
        'attn_pre_g': gain(ks[2]),
        'w_in': nrm(ks[3], (L, D_MODEL, IN_WIDTH), D_MODEL ** -0.5),
        'cmp_pe_k': nrm(ks[4], (L, CMP_LEN, HEAD_DIM), 0.02),
        'cmp_w1_k': nrm(ks[5], (L, CMP_LEN * HEAD_DIM, CMP_HIDDEN), (CMP_LEN * HEAD_DIM) ** -0.5),
        'cmp_w2_k': nrm(ks[6], (L, CMP_HIDDEN, HEAD_DIM), CMP_HIDDEN ** -0.5),
        'cmp_pe_v': nrm(ks[7], (L, CMP_LEN, HEAD_DIM), 0.02),
        'cmp_w1_v': nrm(ks[8], (L, CMP_LEN * HEAD_DIM, CMP_HIDDEN), (CMP_LEN * HEAD_DIM) ** -0.5),
        'cmp_w2_v': nrm(ks[9], (L, CMP_HIDDEN, HEAD_DIM), CMP_HIDDEN ** -0.5),
        'sinks': nrm(ks[10], (L, SWA_HEADS), 1.0),
        'w_o': nrm(ks[11], (L, MIX_WIDTH, D_MODEL), MIX_WIDTH ** -0.5),
        'attn_post_g': gain(ks[12]),
        'mlp_pre_g': gain(ks[13]),
        'w_gate_up': nrm(ks[14], (L, D_MODEL, 2 * D_FF), D_MODEL ** -0.5),
        'conv_w': nrm(ks[15], (L, CONV_W, D_FF), CONV_W ** -0.5),
        'conv_b': nrm(ks[16], (L, D_FF), 0.01),
        'w_down': nrm(ks[17], (L, D_FF, D_MODEL), D_FF ** -0.5),
        'mlp_post_g': gain(ks[18]),
        'w_ple': nrm(ks[19], (L, PLE_DIM, D_MODEL), PLE_DIM ** -0.5),
        'w_ple_gate': nrm(ks[20], (L, D_MODEL, D_MODEL), D_MODEL ** -0.5),
    }


def reference(x, p, attn_pre_g, w_in, cmp_pe_k, cmp_w1_k, cmp_w2_k, cmp_pe_v, cmp_w1_v, cmp_w2_v,
              sinks, w_o, attn_post_g, mlp_pre_g, w_gate_up, conv_w, conv_b, w_down, mlp_post_g,
              w_ple, w_ple_gate):
    B, S, _ = x.shape
    slopes = alibi_slopes()
    swa_slopes = slopes[:SWA_HEADS]
    nsa_slopes = slopes[SWA_HEADS:]
    for i in range(DEPTH):
        h = rmsnorm(x, attn_pre_g[i])
        proj = h @ w_in[i].astype(h.dtype)
        (q_n, kc, vc, ksl, vsl, kwn, vwn, g_n, q_s, k_s, v_s) = jnp.split(proj, IN_SPLIT_POINTS, axis=-1)
        kv_shape = (B, S, NSA_KV_HEADS, HEAD_DIM)
        k_cmp = compress_kv(kc.reshape(kv_shape), cmp_pe_k[i], cmp_w1_k[i], cmp_w2_k[i])
        v_cmp = compress_kv(vc.reshape(kv_shape), cmp_pe_v[i], cmp_w1_v[i], cmp_w2_v[i])
        gates = jax.nn.sigmoid(g_n).reshape(B, S, NSA_HEADS, 3)
        o_nsa = nsa_attention(q_n.reshape(B, S, NSA_HEADS, HEAD_DIM), k_cmp, v_cmp,
                              ksl.reshape(kv_shape), vsl.reshape(kv_shape),
                              kwn.reshape(kv_shape), vwn.reshape(kv_shape), gates, nsa_slopes)
        o_swa = swa_attention(q_s.reshape(B, S, SWA_HEADS, HEAD_DIM),
                              k_s.reshape(B, S, SWA_KV_HEADS, HEAD_DIM),
                              v_s.reshape(B, S, SWA_KV_HEADS, HEAD_DIM), sinks[i], swa_slopes)
        mix = jnp.concatenate([o_nsa, o_swa], axis=-1) @ w_o[i].astype(x.dtype)
        x = x + rmsnorm(mix, attn_post_g[i])
        h = rmsnorm(x, mlp_pre_g[i])
        gu = h @ w_gate_up[i].astype(h.dtype)
        a, u = jnp.split(gu, [D_FF], axis=-1)
        a = causal_dwconv(a, conv_w[i], conv_b[i])
        y = (jax.nn.gelu(a, approximate=True) * u) @ w_down[i].astype(h.dtype)
        x = x + rmsnorm(y, mlp_post_g[i])
        e = p[i] @ w_ple[i].astype(x.dtype)
        x = x + e * jax.nn.sigmoid(x @ w_ple_gate[i].astype(x.dtype))
    return x
```

```python
import numpy as np
from contextlib import ExitStack
import concourse.bass as bass
import concourse.mybir as mybir
from concourse.bass_utils import run_bass_kernel_spmd

F32 = mybir.dt.float32
BF16 = mybir.dt.bfloat16
I32 = mybir.dt.int32
AF = mybir.ActivationFunctionType
ALU = mybir.AluOpType
AX = mybir.AxisListType

N_CORES = 8
D = 1024
HD = 64
DFF = 2816
NCH = DFF // 128
PLE = 256
EPS = 1e-6
NEGM = -30000.0
SLOPES = [2.0 ** (-8.0 * (h + 1.0) / 16.0) for h in range(16)]
SLOPE_SWA = SLOPES[:8]
SLOPE_NSA = SLOPES[8:]

QOFF = 0
RAWOFF = 1024
KSLOFF = 1280
KWNOFF = 1408
KSWOFF = 1536
TMOFF = 1664
WIN_W = 2072


def _bf16_round(v):
    a = np.array([v], dtype=np.float32).view(np.uint32)
    r = ((a + 0x7FFF + ((a >> 16) & 1)) & 0xFFFF0000).astype(np.uint32)
    return float(r.view(np.float32)[0])


def _win_perm():
    perm = []
    for h in range(8):
        perm += list(range(64 * h, 64 * h + 64))
    for h in range(8):
        perm += list(range(1304 + 64 * h, 1304 + 64 * h + 64))
    for g in range(2):
        perm += list(range(512 + 64 * g, 512 + 64 * g + 64))
        perm += list(range(640 + 64 * g, 640 + 64 * g + 64))
    for base in (768, 1024, 1816):
        perm += list(range(base, base + 128))
    perm += list(range(896, 1024)) + list(range(1152, 1280)) + list(range(1944, 2072))
    perm += list(range(1280, 1304))
    assert len(perm) == WIN_W
    return np.array(perm)


ENGS = ("pe", "act", "dve", "pool", "sp")
PSUM_BANKS = frozenset(a + b for a in "ABCD" for b in "01")
NDSEM = 8


class Prog:
    def __init__(self, nc):
        self.nc = nc
        self.ops = {e: [] for e in ENGS}
        self.res_w = {}
        self.res_r = {}
        self.dma_cnt = {"sp": 0, "pool": 0}
        self.ecount = {e: 0 for e in ENGS}
        self.dsem_cnt = {}
        self.dsem_last = {}

    def _deps(self, reads, writes):
        deps = set()
        for r in reads:
            if r in self.res_w:
                deps.add(self.res_w[r])
        for w in writes:
            if w in self.res_w:
                deps.add(self.res_w[w])
            rr = self.res_r.get(w)
            if rr:
                deps.update(rr[0].values())
                deps.update(rr[1])
        return deps

    def _commit(self, me, is_dma, reads, writes):
        for r in reads:
            rr = self.res_r.get(r)
            if rr is None:
                rr = self.res_r[r] = [{}, []]
            if is_dma:
                rr[1].append(me)
            else:
                rr[0][me[0]] = me
        for w in writes:
            self.res_w[w] = me
            self.res_r[w] = [{}, []]

    def op(self, eng, fn, reads=(), writes=()):
        pr = [r for r in reads if r in PSUM_BANKS]
        if pr:
            reads = [r for r in reads if r not in PSUM_BANKS]
            writes = list(writes) + [r for r in pr if r not in writes]
        deps = self._deps(reads, writes)
        me = (eng, len(self.ops[eng]))
        self.ecount[eng] += 1
        self.ops[eng].append((fn, deps, ("e", eng, self.ecount[eng]), False))
        self._commit(me, False, reads, writes)
        return me

    def dma(self, fn, reads=(), writes=(), queue="sp"):
        deps = self._deps(reads, writes)
        n = self.dma_cnt[queue]
        self.dma_cnt[queue] = n + 1
        j = n % NDSEM
        key = (queue, j)
        if key in self.dsem_last:
            deps.add(self.dsem_last[key])
        c = self.dsem_cnt.get(key, 0) + 1
        self.dsem_cnt[key] = c
        me = (queue, len(self.ops[queue]))
        self.ops[queue].append((fn, deps, ("d", queue, j, 16 * c), True))
        self.dsem_last[key] = me
        self._commit(me, True, reads, writes)
        return me

    def emit(self, sync_same_engine=True):
        nc = self.nc
        with ExitStack() as st:
            esem = {e: st.enter_context(nc.semaphore("s_" + e)) for e in ENGS}
            dsem = {}
            for q in ("sp", "pool"):
                for j in range(NDSEM):
                    dsem[(q, j)] = st.enter_context(nc.semaphore("d_%s%d" % (q, j)))
            block = st.enter_context(nc.Block())

            def tok_sem(tok):
                if tok[0] == "e":
                    return ("e", tok[1]), esem[tok[1]], tok[2]
                return ("d", tok[1], tok[2]), dsem[(tok[1], tok[2])], tok[3]

            def body(eng_name):
                def f(eng):
                    waited = {}
                    for (fn, deps, tok, is_dma) in self.ops[eng_name]:
                        need = {}
                        for (e2, i2) in deps:
                            t2 = self.ops[e2][i2][2]
                            if e2 == eng_name and t2[0] == "e":
                                if eng_name == "pe" or not sync_same_engine:
                                    continue
                            k, s, v = tok_sem(t2)
                            if waited.get(k, 0) >= v:
                                continue
                            if k not in need or need[k][1] < v:
                                need[k] = (s, v)
                        for k, (s, v) in need.items():
                            eng.wait_ge(s, v)
                            waited[k] = v
                        ins = fn(eng)
                        if is_dma:
                            ins.then_inc(dsem[(tok[1], tok[2])], 16)
                        else:
                            ins.then_inc(esem[eng_name], 1)
                    if eng_name in ("sp", "pool"):
                        for j in range(NDSEM):
                            c = self.dsem_cnt.get((eng_name, j), 0)
                            if c:
                                eng.wait_ge(dsem[(eng_name, j)], 16 * c)
                return f

            block.tensor(body("pe"))
            block.scalar(body("act"))
            block.vector(body("dve"))
            block.gpsimd(body("pool"))
            block.sync(body("sp"))


class Ctx:
    def __init__(self, nc, P):
        self.nc = nc
        self.P = P

    def mm(self, out, lhsT, rhs, start, stop, r, w):
        self.P.op("pe", lambda e: e.matmul(out, lhsT=lhsT, rhs=rhs, start=start, stop=stop), r, w)

    def tr(self, out, in_, ident, r, w):
        self.P.op("pe", lambda e: e.transpose(out, in_, ident), r, w)

    def act(self, out, in_, func, r, w, bias=None, scale=None, accum=None):
        kw = {}
        if bias is not None:
            kw["bias"] = bias
        if scale is not None:
            kw["scale"] = scale
        if accum is not None:
            kw["accum_out"] = accum
        self.P.op("act", lambda e: e.activation(out=out, in_=in_, func=func, **kw), r, w)

    def v(self, eng, name, r, w, *args, **kw):
        self.P.op(eng, lambda e: getattr(e, name)(*args, **kw), r, w)

    def dma(self, out, in_, r, w, queue="sp", **kw):
        self.P.dma(lambda e: e.dma_start(out=out, in_=in_, **kw), r, w, queue=queue)


def bcast(ap, shape):
    return ap.broadcast_to(shape)


import os
_STOP = os.environ.get("KSTOP", "")


class _StopBuild(Exception):
    pass


def _stop(tag):
    if _STOP == tag:
        raise _StopBuild()


def pass_a(nc, dr, NSEQ, S):
    NT = S // 512
    NKT = S // 128
    with ExitStack() as st:
        def sb(name, shape, dt):
            return st.enter_context(nc.sbuf_tensor(name, shape, dt))

        def ps(name):
            return st.enter_context(nc.psum_tensor(name, [128, 1024], F32))

        P = Prog(nc)
        C = Ctx(nc, P)

        winT = sb("winT", [128, 8, WIN_W], BF16)
        woT = sb("woT", [128, 8, 1024], BF16)
        w1kv = sb("w1kv", [128, 32, 128], BF16)
        w2kv = sb("w2kv", [128, 2, 64], BF16)
        peT = sb("peT", [128, 32], BF16)
        hbias = sb("hbias", [128, 2], F32)
        gpre = sb("gpre", [128, 1024], BF16)
        gpost = sb("gpost", [128, 1024], F32)
        ident = sb("ident", [128, 128], BF16)
        identf = sb("identf", [128, 128], F32)
        causT = sb("causT", [128, 128], BF16)
        edgeT = sb("edgeT", [128, 128], BF16)
        maskf = sb("maskf", [128, 128], F32)
        posi = sb("posi", [128, 36], I32)
        posf = sb("posf", [128, 36], F32)
        biasN = sb("biasN", [128, 8, 36], F32)
        biasS = sb("biasS", [128, 8, 2], F32)
        coli = sb("coli", [128, 1], I32)
        colf = sb("colf", [128, 1], F32)
        colb = sb("colb", [128, 8], BF16)
        dl = sb("dl", [128, 8], F32)
        sinkb = sb("sinkb", [128, 8], F32)
        sinkterm = sb("sinkterm", [128, 8], F32)
        cmpmask = sb("cmpmask", [128, 9], F32)
        Wadj = sb("Wadj", [128, 128], F32)
        qT = sb("qT", [128, 16, 512], BF16)
        kslT = sb("kslT", [128, 2, S], BF16)
        cq = sb("cq", [128, 8, 128], BF16)
        kc2 = sb("kc2", [128, 256], BF16)
        vsl = sb("vsl", [128, NKT, 2, 66], BF16)
        kwnT = sb("kwnT", [64, 2, 1024], BF16)
        vwn = sb("vwn", [128, 8, 2, 66], BF16)
        kswT = sb("kswT", [128, 2, 1024], BF16)
        vsw = sb("vsw", [128, 8, 2, 66], BF16)
        kcT = sb("kcT", [128, 2, 256], BF16)
        vcT = sb("vcT", [64, 2, 256], BF16)
        vcm = sb("vcm", [128, 2, 2, 64], BF16)
        raw = [sb("raw%d" % g, [128, 528], BF16) for g in range(2)]
        hid = sb("hid", [128, 32], BF16)
        xs = [sb("xs%d" % i, [128, 1024], F32) for i in range(2)]
        yts = [sb("yt%d" % i, [128, 1024], F32) for i in range(2)]
        tok = sb("tok", [128, 4, 1024], BF16)
        hT = sb("hT", [128, 8, 512], BF16)
        ss = sb("ss", [128, 8], F32)
        st2 = sb("st2", [128, 8], F32)
        rstd = sb("rstd", [128, 8], F32)
        gates = sb("gates", [128, 4, 24], F32)
        oacc = sb("oacc", [128, 4, 8, 64], F32)
        tmpo = [sb("tmpo%d" % i, [128, 4, 64], F32) for i in range(2)]
        tmpc = sb("tmpc", [128, 4, 64], F32)
        ef = sb("ef", [128, 4, 256], F32)
        pf = sb("pf", [128, 4, 256], F32)
        pb = sb("pb", [128, 4, 256], BF16)
        ef_flat = ef[:, :, :].rearrange("p h n -> p (h n)")
        pf_flat = pf[:, :, :].rearrange("p h n -> p (h n)")
        Indf = ef_flat[64:128, 0:512]
        rowf = ef_flat[:, 512:1024]
        rowi = pf_flat.bitcast(I32)[:, 0:512]
        pT = sb("pT", [128, 8, 128], BF16)
        psum4 = sb("psum4", [128, 264], F32)
        impA = sb("impA", [128, 64], F32)
        imp = sb("imp", [128, 64], F32)
        imp2 = sb("imp2", [128, 64], F32)
        m8 = sb("m8", [128, 16], F32)
        mb = sb("mb", [128, 128], BF16)
        mx = sb("mx", [128, 4], F32)
        negm = sb("negm", [128, 4], F32)
        sm = sb("sm", [128, 4], F32)
        rs = sb("rs", [128, 4], F32)
        sd = [sb("sd%d" % i, [128, 8], F32) for i in range(2)]
        PT = [sb("PT%d" % i, [128, 512], BF16) for i in range(3)]
        PW = [sb("PW%d" % i, [128, 4, 128], BF16) for i in range(4)]

        psA, psB, psC, psD = ps("psA"), ps("psB"), ps("psC"), ps("psD")
        PSB = {"A": psA, "B": psB, "C": psC, "D": psD}

        def bank(nm):
            t = PSB[nm[0]]
            k = int(nm[1])
            return t[:, 512 * k:512 * (k + 1)]

        pq = "pool"
        C.v(pq, "memset", [], ["identf"], identf[:], 1.0)
        C.v(pq, "affine_select", ["identf"], ["identf"], out=identf[:], in_=identf[:], pattern=[[-1, 128]],
            compare_op=ALU.is_equal, fill=0.0, base=0, channel_multiplier=1)
        C.v("dve", "tensor_copy", ["identf"], ["ident"], out=ident[:], in_=identf[:])
        C.v(pq, "memset", [], ["maskf"], maskf[:], 0.0)
        C.v(pq, "affine_select", ["maskf"], ["maskf"], out=maskf[:], in_=maskf[:], pattern=[[1, 128]],
            compare_op=ALU.is_ge, fill=NEGM, base=0, channel_multiplier=-1)
        C.v("dve", "tensor_copy", ["maskf"], ["causT"], out=causT[:], in_=maskf[:])
        C.v(pq, "memset", ["maskf"], ["maskf"], maskf[:], 0.0)
        C.v(pq, "affine_select", ["maskf"], ["maskf"], out=maskf[:], in_=maskf[:], pattern=[[-1, 128]],
            compare_op=ALU.is_ge, fill=NEGM, base=-1, channel_multiplier=1)
        C.v("dve", "tensor_copy", ["maskf"], ["edgeT"], out=edgeT[:], in_=maskf[:])
        for pc in range(S // 512):
            C.v(pq, "memset", ["ef"], ["ef"], Indf, 1.0)
            C.v(pq, "affine_select", ["ef"], ["ef"], out=Indf, in_=Indf, pattern=[[1, 512]],
                compare_op=ALU.is_ge, fill=0.0, base=512 * pc, channel_multiplier=-64)
            C.v(pq, "affine_select", ["ef"], ["ef"], out=Indf, in_=Indf, pattern=[[-1, 512]],
                compare_op=ALU.is_ge, fill=0.0, base=63 - 512 * pc, channel_multiplier=64)
            for g in range(2):
                C.v("dve", "tensor_copy", ["ef"], ["Ind"], out=kslT[64:128, g, 512 * pc:512 * (pc + 1)], in_=Indf)
        C.v(pq, "iota", [], ["posi"], posi[:], pattern=[[128, 36]], base=-32 * 128, channel_multiplier=1)
        C.v("dve", "tensor_copy", ["posi"], ["posf"], out=posf[:], in_=posi[:])
        for h in range(8):
            C.v("dve", "tensor_scalar", ["posf"], ["biasN"], out=biasN[:, h, :], in0=posf[:],
                scalar1=float(SLOPE_NSA[h]), scalar2=None, op0=ALU.mult)
            C.v("dve", "tensor_scalar", ["posf"], ["biasS"], out=biasS[:, h, :], in0=posf[:, 31:33],
                scalar1=float(SLOPE_SWA[h]), scalar2=None, op0=ALU.mult)
        C.v(pq, "memset", [], ["qTc"] + ["qT%d" % h for h in range(16)] + ["maskT%d.%d" % (g_, s_) for g_ in range(2) for s_ in range(4)], qT[:], 0.0)
        C.v(pq, "memset", [], ["cq"], cq[:], 0.0)
        C.v(pq, "memset", [], ["mb"], mb[:], 0.0)
        for h in range(8):
            hi = _bf16_round(SLOPE_NSA[h])
            lo = SLOPE_NSA[h] - hi
            C.v(pq, "memset", ["cq"], ["cq"], cq[0:1, h, :], hi)
            C.v(pq, "memset", ["cq"], ["cq"], cq[32:33, h, :], lo)
        C.v(pq, "iota", [], ["pf"], rowi[64:65, :], pattern=[[0, 4], [1, 128]], base=0, channel_multiplier=0)
        C.v("dve", "tensor_copy", ["pf"], ["ef"], out=rowf[64:65, :], in_=rowi[64:65, :])
        C.v(pq, "iota", [], ["coli"], coli[:], pattern=[[0, 1]], base=0, channel_multiplier=1)
        C.v("dve", "tensor_copy", ["coli"], ["colf"], out=colf[:], in_=coli[:])
        C.dma(sinkb[:], dr["sinks"].partition_broadcast(128), [], ["sinkb"])
        for h in range(8):
            C.v("dve", "tensor_scalar", ["ef", "qTc"], ["qTc"], out=qT[64:65, 8 + h, :], in0=rowf[64:65, :],
                scalar1=-float(SLOPE_SWA[h]), scalar2=None, op0=ALU.mult)
            C.v("dve", "tensor_scalar", ["colf"], ["colb"], out=colb[:, h:h + 1], in0=colf[:],
                scalar1=-float(SLOPE_SWA[h]), scalar2=None, op0=ALU.mult)
            C.v("dve", "scalar_tensor_tensor", ["colf", "colb"], ["dl"], out=dl[:, h:h + 1], in0=colf[:],
                scalar=float(SLOPE_SWA[h]), in1=colb[:, h:h + 1], op0=ALU.mult, op1=ALU.add)
            C.act(sinkterm[:, h:h + 1], dl[:, h:h + 1], AF.Exp, ["dl", "sinkb"], ["sinkterm"],
                  bias=sinkb[:, h:h + 1])
        C.v(pq, "memset", [], ["kcT"], kcT[:], 0.0)
        C.v(pq, "memset", [], ["kc2"], kc2[:], 0.0)
        C.v(pq, "iota", ["pf"], ["pf"], rowi[0:1, 0:256], pattern=[[16, 256]], base=0, channel_multiplier=0)
        C.v(pq, "iota", ["pf"], ["pf"], rowi[32:33, 0:256], pattern=[[16, 256]], base=0, channel_multiplier=0)
        C.v("dve", "tensor_copy", ["pf", "kc2"], ["kc2"], out=kc2[0:1, :], in_=rowi[0:1, 0:256])
        C.v("dve", "tensor_copy", ["pf", "kc2"], ["kc2"], out=kc2[32:33, :], in_=rowi[32:33, 0:256])
        C.v(pq, "memset", [], ["vcT"], vcT[:], 0.0)
        C.v(pq, "memset", [], ["cmpmask"], cmpmask[:], 0.0)
        C.v(pq, "affine_select", ["cmpmask"], ["cmpmask"], out=cmpmask[:], in_=cmpmask[:], pattern=[[-16, 9]],
            compare_op=ALU.is_ge, fill=NEGM, base=1, channel_multiplier=1)
        C.v(pq, "memset", [], ["Wadj"], Wadj[:], 0.0)
        C.v(pq, "memset", ["Wadj"], ["Wadj"], Wadj[0:64, 63:65], 1.0e6)
        C.v(pq, "memset", ["Wadj"], ["Wadj"], Wadj[0:64, 65:128], -1.0e30)
        C.v(pq, "memset", ["Wadj"], ["Wadj"], Wadj[64:128, 64:66], 1.0e6)
        C.v(pq, "memset", ["Wadj"], ["Wadj"], Wadj[64:128, 66:128], -1.0e30)
        C.v(pq, "memset", [], ["vslc"], vsl[:, :, :, 64:65], 1.0)
        C.v(pq, "memset", [], ["vwnc"], vwn[:, :, :, 64:65], 1.0)
        C.v(pq, "memset", [], ["vswc"], vsw[:, :, :, 64:65], 1.0)
        C.v(pq, "memset", [], ["kswc"], kswT[64:65, :, :], 1.0)
        for g in range(2):
            C.v(pq, "memset", [], ["raw%d" % g], raw[g][:], 0.0)

        if _STOP == "const":
            P.emit()
            return
        jd = sb("jdummy", [128, 8], F32)

        def join(name, n, col):
            C.v("pool", "memset", ["%s.%d" % (name, i) for i in range(n)], [name], jd[:, col:col + 1], 0.0)

        wi = 0
        for c in range(8):
            for (a, b) in ((0, 1036), (1036, WIN_W)):
                C.dma(winT[:, c, a:b], dr["w_in"][128 * c:128 * (c + 1), a:b], [], ["winT.%d" % wi], queue="pool")
                wi += 1
        join("winT", wi, 0)
        C.dma(gpre[:], dr["attn_pre_g"].partition_broadcast(128), [], ["gpre"], queue="pool")
        C.dma(gpost[:], dr["attn_post_g"].partition_broadcast(128), [], ["gpost"])
        C.dma(w1kv[0:64, :, :], dr["cmp_w1_k"].rearrange("(l d) h -> d l h", d=64), [], ["w1kv.0"], queue="pool")
        C.dma(w1kv[64:128, :, :], dr["cmp_w1_v"].rearrange("(l d) h -> d l h", d=64), [], ["w1kv.1"], queue="pool")
        join("w1kv", 2, 1)
        C.dma(w2kv[:, 0, :], dr["cmp_w2_k"], [], ["w2kv.0"], queue="pool")
        C.dma(w2kv[:, 1, :], dr["cmp_w2_v"], [], ["w2kv.1"], queue="pool")
        join("w2kv", 2, 2)
        C.dma(peT[0:64, :], dr["cmp_pe_kT"], [], ["peT.0"], queue="pool")
        C.dma(peT[64:128, :], dr["cmp_pe_vT"], [], ["peT.1"], queue="pool")
        join("peT", 2, 3)
        for c in range(8):
            C.dma(woT[:, c, :], dr["w_o"][128 * c:128 * (c + 1), :], [], ["woT.%d" % c], queue="pool")
        join("woT", 8, 4)
        for kv in range(2):
            rows = slice(64 * kv, 64 * kv + 64)
            for l in range(32):
                C.mm(bank("A1")[:, kv:kv + 1], w1kv[rows, l, :], peT[rows, l:l + 1], l == 0, l == 31,
                     ["w1kv", "peT"], ["A1"])
            C.v("dve", "tensor_copy", ["A1"], ["hbias"], out=hbias[:, kv:kv + 1], in_=bank("A1")[:, kv:kv + 1])

        if _STOP == "weights":
            P.emit()
            return
        sel_rot = [0]
        acc_rot = [0]
        ev_rot = [0]

        def evac(out, in_, r, w, scale=None):
            ev_rot[0] ^= 1
            if ev_rot[0]:
                if scale is None:
                    C.act(out, in_, AF.Copy, r, w)
                else:
                    C.act(out, in_, AF.Copy, r, w, scale=scale)
            else:
                if scale is None:
                    C.v("dve", "tensor_copy", r, w, out=out, in_=in_)
                else:
                    C.v("dve", "tensor_scalar", r, w, out=out, in0=in_, scalar1=scale, scalar2=None, op0=ALU.mult)

        try:
          for s in range(NSEQ):
            C.v("pool", "memset", ["psum4"], ["psum4"], psum4[:], 0.0)
            C.v("pool", "memset", ["pb"], ["pb"], pb[:], 0.0)
            for T in range(NT):
                t0 = 512 * T
                row0 = s * S + t0
                slot = T % 2
                for sub in range(4):
                    b = sub % 2
                    xr = "xs%d" % b
                    C.dma(xs[b][:], dr["x"][row0 + 128 * sub:row0 + 128 * (sub + 1), :], [], [xr])
                    C.act(tok[:, sub, :], xs[b][:], AF.Square, [xr], ["tok%d" % sub, "ss"], accum=ss[:, sub:sub + 1])
                    C.act(st2[:, sub:sub + 1], ss[:, sub:sub + 1], AF.Sqrt, ["ss"], ["st2"], scale=1.0 / D, bias=EPS)
                    C.v("dve", "reciprocal", ["st2"], ["rstd"], out=rstd[:, sub:sub + 1], in_=st2[:, sub:sub + 1])
                    C.v("dve", "scalar_tensor_tensor", [xr, "rstd", "gpre"], ["tok%d" % sub], out=tok[:, sub, :],
                        in0=xs[b][:], scalar=rstd[:, sub:sub + 1], in1=gpre[:], op0=ALU.mult, op1=ALU.mult)
                    tbk = "A%d" % (sub % 2)
                    tpb = bank(tbk).bitcast(BF16).rearrange("p (c t) -> p c t", c=8)
                    for c in range(8):
                        C.tr(tpb[:, c, :], tok[:, sub, 128 * c:128 * (c + 1)], ident[:], ["tok%d" % sub, "ident"], [tbk])
                    evac(hT[:, :, 128 * sub:128 * (sub + 1)], tpb, [tbk], ["hT%d" % sub])
                hTall = ["hT0", "hT1", "hT2", "hT3"]
                _stop("s1")

                prj = [0]

                def proj_fm(col0, M, out_ap, wres, scale=None):
                    bk = ("B0", "B1", "D0", "D1")[prj[0]]
                    prj[0] = (prj[0] + 1) % 4
                    o = bank(bk)[0:M, :]
                    for c in range(8):
                        C.mm(o, winT[:, c, col0:col0 + M], hT[:, c, :], c == 0, c == 7, ["winT"] + hTall, [bk])
                    evac(out_ap, o, [bk], wres, scale=scale)

                for h in range(16):
                    proj_fm(QOFF + 64 * h, 64, qT[0:64, h, :], ["qT%d" % h], scale=0.125)
                _stop("s2a")
                for g in range(2):
                    proj_fm(RAWOFF + 128 * g, 128, raw[g][:, 16:528], ["raw%d" % g])
                    proj_fm(KSLOFF + 64 * g, 64, kslT[0:64, g, t0:t0 + 512], ["ksl%d.%d" % (g, T)])
                    proj_fm(KWNOFF + 64 * g, 64, kwnT[:, g, 512 * slot:512 * (slot + 1)], ["kwn%d.%d" % (g, slot)])
                    proj_fm(KSWOFF + 64 * g, 64, kswT[0:64, g, 512 * slot:512 * (slot + 1)], ["ksw%d.%d" % (g, slot)])
                _stop("s2b")
                for sub in range(4):
                    bk = "C%d" % (sub % 2)
                    o = bank(bk)[:, 0:408]
                    for c in range(8):
                        C.mm(o, hT[:, c, 128 * sub:128 * (sub + 1)], winT[:, c, TMOFF:TMOFF + 408], c == 0, c == 7,
                             ["winT", "hT%d" % sub], [bk])
                    kt = 4 * T + sub
                    _stop("s2c")
                    evac(vsl[:, kt, :, 0:64], o[:, 0:128].rearrange("p (g e) -> p g e", g=2), [bk], ["vsl.%d" % T])
                    _stop("s2d")
                    evac(vwn[:, 4 * slot + sub, :, 0:64], o[:, 128:256].rearrange("p (g e) -> p g e", g=2), [bk],
                         ["vwn.%d" % slot])
                    evac(vsw[:, 4 * slot + sub, :, 0:64], o[:, 256:384].rearrange("p (g e) -> p g e", g=2), [bk],
                         ["vsw.%d" % slot])
                    _stop("s2e")
                    C.act(gates[:, sub, :], o[:, 384:408], AF.Sigmoid, [bk], ["gates"])
                    _stop("s2f")

                _stop("s2")
                if T == 0:
                    NB, n0, cbase = 31, 0, 16
                else:
                    NB, n0, cbase = 32, 32 * T - 1, 0
                for g in range(2):
                    for kv in range(2):
                        rows = slice(64 * kv, 64 * kv + 64)
                        o = bank("A1")[:, 0:NB]
                        for l in range(32):
                            C.mm(o, w1kv[rows, l, :], raw[g][rows, cbase + l:cbase + l + 16 * (NB - 1) + 1:16],
                                 l == 0, l == 31, ["w1kv", "raw%d" % g], ["A1"])
                        C.act(hid[:, 0:NB], o, AF.Gelu_apprx_tanh, ["A1", "hbias"], ["hid"], bias=hbias[:, kv:kv + 1])
                        o2 = bank("A1")[0:64, 64:64 + NB]
                        C.mm(o2, w2kv[:, kv, :], hid[:, 0:NB], True, True, ["w2kv", "hid"], ["A1"])
                        if kv == 0:
                            evac(kcT[0:64, g, n0:n0 + NB], o2, ["A1"], ["kcT"])
                        else:
                            evac(vcT[:, g, n0:n0 + NB], o2, ["A1"], ["vcT"])
                    C.v("pool", "tensor_copy", ["raw%d" % g], ["raw%d" % g], out=raw[g][:, 0:16], in_=raw[g][:, 512:528])
                ncmax = 32 * T + 31
                nnt = 2 if ncmax > 128 else 1
                vtp = bank("A1").bitcast(BF16)
                for g in range(2):
                    for nt in range(nnt):
                        C.tr(vtp[:, 0:64], vcT[:, g, 128 * nt:128 * (nt + 1)], ident[0:64, 0:64], ["vcT", "ident"], ["A1"])
                        evac(vcm[:, nt, g, :], vtp[:, 0:64], ["A1"], ["vcm"])


                def cmp_s1(sub, g):
                    a = 4 * T + sub
                    nc_ = min(8 * a + 7, 255)
                    Sc = psD[:, :].rearrange("p (h n) -> p h n", h=4)
                    pres = ["D0", "D1"]
                    C.v("pool", "memset", ["psum4"], ["psum4"], psum4[:, 1 + nc_:264], 0.0)
                    if nc_ < 256:
                        C.v("pool", "memset", ["pb"], ["pb"], pb[:, :, nc_:256], 0.0)
                    for h in range(4):
                        C.mm(Sc[:, h, 0:nc_], qT[0:64, 4 * g + h, 128 * sub:128 * (sub + 1)], kcT[0:64, g, 0:nc_],
                             True, False, ["qT%d" % (4 * g + h), "kcT"], [pres[h // 2]])
                        C.mm(Sc[:, h, 0:nc_], cq[0:33, 4 * g + h, :], kc2[0:33, 0:nc_],
                             False, True, ["cq", "kc2"], [pres[h // 2]])
                    if a == 0:
                        c0, m0 = 0, 2
                    else:
                        c0, m0 = 8 * a - 2, 0
                    wdt = nc_ - c0
                    C.v("dve", "tensor_tensor", pres + ["cmpmask"], pres, out=Sc[:, :, c0:nc_], in0=Sc[:, :, c0:nc_],
                        in1=bcast(cmpmask[:, m0:m0 + wdt].unsqueeze(1), [128, 4, wdt]), op=ALU.add)
                    C.v("dve", "tensor_reduce", pres, ["mx"], out=mx[:], in_=Sc[:, :, 0:nc_], axis=AX.X, op=ALU.max)
                    C.v("dve", "tensor_scalar", ["mx"], ["negm"], out=negm[:], in0=mx[:], scalar1=-1000.0, scalar2=-1.0,
                        op0=ALU.max, op1=ALU.mult)
                    for h in range(4):
                        C.act(ef[:, h, 0:nc_], Sc[:, h, 0:nc_], AF.Exp, [pres[h // 2], "negm"], ["ef", "sm"],
                              bias=negm[:, h:h + 1], accum=sm[:, h:h + 1])
                    C.v("dve", "tensor_scalar", ["sm"], ["rs"], out=rs[:], in0=sm[:], scalar1=1.0e-30, scalar2=None,
                        op0=ALU.max)
                    C.v("dve", "reciprocal", ["rs"], ["rs"], out=rs[:], in_=rs[:])
                    C.v("dve", "tensor_tensor", ["ef", "rs"], ["pf"], out=pf[:, :, 0:nc_], in0=ef[:, :, 0:nc_],
                        in1=bcast(rs[:, :].unsqueeze(2), [128, 4, nc_]), op=ALU.mult)
                    C.v("pool", "tensor_copy", ["pf"], ["pb"], out=pb[:, :, 0:nc_], in_=pf[:, :, 0:nc_])
                    C.v("dve", "tensor_reduce", ["pf"], ["psum4"], out=psum4[:, 1:1 + nc_],
                        in_=pf[:, :, 0:nc_].rearrange("p h n -> p n h"), axis=AX.X, op=ALU.add)
                    C.v("dve", "tensor_reduce", ["psum4"], ["impA"], out=impA[:],
                        in_=psum4[:, 0:256].rearrange("p (j f) -> p j f", f=4), axis=AX.X, op=ALU.add)
                    C.v("dve", "tensor_tensor", ["impA", "psum4"], ["imp"], out=imp[:], in0=impA[:],
                        in1=psum4[:, 4:260:4], op=ALU.add)
                    C.v("dve", "tensor_tensor", ["imp", "Wadj"], ["imp"], out=imp[:], in0=imp[:],
                        in1=Wadj[:, 64 - 2 * a:128 - 2 * a], op=ALU.add)
                    C.v("dve", "tensor_scalar", ["imp"], ["imp"], out=imp[:, 0:1], in0=imp[:, 0:1], scalar1=1.0e6,
                        scalar2=None, op0=ALU.add)
                    C.v("dve", "max", ["imp"], ["m8"], out=m8[:, 0:8], in_=imp[:])
                    C.v("dve", "match_replace", ["imp", "m8"], ["imp2"], out=imp2[:], in_to_replace=m8[:, 0:8],
                        in_values=imp[:], imm_value=-3.0e38)
                    C.v("dve", "max", ["imp2"], ["m8"], out=m8[:, 8:16], in_=imp2[:])
                    C.v("dve", "tensor_scalar", ["imp", "m8"], ["mb"], out=mb[:, 64:128], in0=imp[:], scalar1=m8[:, 15:16],
                        scalar2=NEGM, op0=ALU.is_lt, op1=ALU.mult)

                def cmp_s3(sub, g):
                    a = 4 * T + sub
                    nc_ = min(8 * a + 7, 255)
                    mtp = bank("A0").bitcast(BF16)
                    C.tr(mtp[:, 0:128], mb[:], ident[:], ["mb", "ident"], ["A0"])
                    evac(qT[64:128, 4 * g:4 * g + 4, 128 * sub:128 * (sub + 1)],
                         bcast(mtp[64:128, 0:128].unsqueeze(1), [64, 4, 128]), ["A0"], ["maskT%d.%d" % (g, sub)])
                    nn = 2 if nc_ > 128 else 1
                    ptp = bank("A1").bitcast(BF16).rearrange("p (k q) -> p k q", k=8)
                    for h in range(4):
                        for nt in range(nn):
                            C.tr(ptp[:, 2 * h + nt, :], pb[:, h, 128 * nt:128 * (nt + 1)], ident[:], ["pb", "ident"], ["A1"])
                    if nn == 2:
                        evac(pT[:], ptp, ["A1"], ["pT"])
                    else:
                        evac(pT[:, 0:8:2, :], ptp[:, 0:8:2, :], ["A1"], ["pT"])
                    oc = bank("A0")[:, 128:384].rearrange("p (h e) -> p h e", h=4)
                    first = True
                    for h in range(4):
                        for nt in range(nn):
                            C.mm(oc[:, h, :], pT[:, 2 * h + nt, :], vcm[:, nt, g, :], first, (h == 3 and nt == nn - 1),
                                 ["pT", "vcm"], ["A0"])
                            first = False
                    C.v("dve", "tensor_tensor", ["A0", "gates"], ["tmpc"], out=tmpc[:],
                        in0=oc, in1=bcast(gates[:, sub, 12 * g:12 * g + 12:3].unsqueeze(2), [128, 4, 64]), op=ALU.mult)
                    C.v("pool", "tensor_tensor", ["tmpc", "oacc%d" % g], ["oacc%d" % g], out=oacc[:, sub, 4 * g:4 * g + 4, :],
                        in0=oacc[:, sub, 4 * g:4 * g + 4, :], in1=tmpc[:], op=ALU.add)

                def swa_A(hs):
                    gs = hs // 4
                    qh = 8 + hs
                    pk = "B" if hs % 2 == 0 else "C"
                    pt_ = PSB[pk]
                    Sw = [pt_[:, 0:512].rearrange("p (s q) -> p s q", s=4), pt_[:, 512:1024].rearrange("p (s q) -> p s q", s=4)]
                    firstb = [True, True]
                    for sub in range(4):
                        a = 4 * T + sub
                        for which in range(2):
                            kt = a - 1 + which
                            if kt < 0:
                                continue
                            kslot = (kt // 4) % 2
                            kcol = 512 * kslot + 128 * (kt % 4)
                            bk = "%s%d" % (pk, which)
                            C.mm(Sw[which][:, sub, :], kswT[0:65, gs, kcol:kcol + 128], qT[0:65, qh, 128 * sub:128 * (sub + 1)],
                                 firstb[which], False, ["ksw%d.%d" % (gs, kslot), "kswc", "qT%d" % qh, "qTc"], [bk])
                            firstb[which] = False
                            C.mm(Sw[which][:, sub, :], ident[:], (edgeT if which == 0 else causT)[:], False, sub == 3,
                                 ["ident", "edgeT", "causT"], [bk])

                def swa_BC(hs):
                    gs = hs // 4
                    pk = "B" if hs % 2 == 0 else "C"
                    pt_ = PSB[pk]
                    Sw = [pt_[:, 0:512].rearrange("p (s q) -> p s q", s=4), pt_[:, 512:1024].rearrange("p (s q) -> p s q", s=4)]
                    pw = PW[2 * (hs % 2):2 * (hs % 2) + 2]
                    pwr = ["PW%d" % (2 * (hs % 2)), "PW%d" % (2 * (hs % 2) + 1)]
                    s_lo = 1 if T == 0 else 0
                    C.act(pw[0][:, s_lo:4, :], Sw[0][:, s_lo:4, :], AF.Exp, [pk + "0", "biasS"], [pwr[0]], bias=biasS[:, hs, 0:1])
                    C.act(pw[1][:], Sw[1][:], AF.Exp, [pk + "1", "biasS"], [pwr[1]], bias=biasS[:, hs, 1:2])
                    ab = "D%d" % (hs % 2)
                    acc = bank(ab)[:, 0:260].rearrange("p (s e) -> p s e", s=4)
                    first = True
                    for sub in range(4):
                        a = 4 * T + sub
                        for which in range(2):
                            kt = a - 1 + which
                            if kt < 0:
                                continue
                            kslot = (kt // 4) % 2
                            C.mm(acc[:, sub, :], pw[which][:, sub, :], vsw[:, 4 * kslot + kt % 4, gs, 0:65], first,
                                 (sub == 3 and which == 1), [pwr[which], "vsw.%d" % kslot, "vswc"], [ab])
                            first = False
                    sdd = sd[hs % 2]
                    sdr = "sd%d" % (hs % 2)
                    C.v("dve", "tensor_tensor", [ab, "sinkterm"], [sdr], out=sdd[:, 0:4], in0=acc[:, :, 64],
                        in1=bcast(sinkterm[:, hs:hs + 1], [128, 4]), op=ALU.add)
                    C.v("dve", "reciprocal", [sdr], [sdr], out=sdd[:, 4:8], in_=sdd[:, 0:4])
                    C.v("dve", "tensor_tensor", [ab, sdr], ["tok0", "tok1", "tok2", "tok3"], out=tok[:, :, 512 + 64 * hs:512 + 64 * hs + 64],
                        in0=acc[:, :, 0:64], in1=bcast(sdd[:, 4:8].unsqueeze(2), [128, 4, 64]), op=ALU.mult)

                _stop("s3")
                C.v("pool", "memset", ["oacc0", "oacc1"], ["oacc0", "oacc1"], oacc[:], 0.0)
                swa_A(0)
                for hs in range(8):
                    if hs + 1 < 8:
                        swa_A(hs + 1)
                    swa_BC(hs)
                _stop("s5")

                def nsa_items(h, kts, kind, ab):
                    g = h // 4
                    items = []
                    nk = len(kts)
                    for ki, kt in enumerate(kts):
                        c = kt - 4 * T
                        if kind == "win" and c < 0:
                            e = c + 4
                            q0, q1 = 0, 128 * (e + 1)
                            mcol, mtile = 128 * e, edgeT
                        elif c >= 0:
                            q0, q1 = 128 * c, 512
                            mcol, mtile = 128 * c, causT
                        else:
                            q0, q1 = 0, 512
                            mcol, mtile = None, None
                        if kind == "sel":
                            kap = kslT[0:128, g, 128 * kt:128 * (kt + 1)]
                            kres = "ksl%d.%d" % (g, kt // 4)
                            vap = vsl[:, kt, g, 0:65]
                            vres = ["vsl.%d" % (kt // 4), "vslc"]
                        else:
                            kslot = (kt // 4) % 2
                            kcol = 512 * kslot + 128 * (kt % 4)
                            kap = kwnT[:, g, kcol:kcol + 128]
                            kres = "kwn%d.%d" % (g, kslot)
                            vap = vwn[:, 4 * kslot + kt % 4, g, 0:65]
                            vres = ["vwn.%d" % kslot, "vwnc"]
                        items.append(dict(h=h, g=g, kind=kind, kt=kt, c=c, q0=q0, q1=q1, mcol=mcol, mtile=mtile, kap=kap,
                                          kres=kres, vap=vap, vres=vres, ab=ab, first=(ki == 0), last=(ki == nk - 1)))
                    return items

                def nsa_A(it, i):
                    bk = "B%d" % (i % 2)
                    Sb = bank(bk)
                    q0, q1, h, g = it["q0"], it["q1"], it["h"], it["g"]
                    sel = it["kind"] == "sel"
                    mcol = it["mcol"]
                    if sel:
                        C.mm(Sb[:, q0:q1], it["kap"], qT[0:128, h, q0:q1], True, mcol is None,
                             [it["kres"], "qT%d" % h, "Ind"] + ["maskT%d.%d" % (g, j) for j in range(q0 // 128, q1 // 128)], [bk])
                    else:
                        C.mm(Sb[:, q0:q1], it["kap"], qT[0:64, h, q0:q1], True, mcol is None, [it["kres"], "qT%d" % h], [bk])
                    if mcol is not None:
                        C.mm(Sb[:, mcol:mcol + 128], ident[:], it["mtile"][:], False, True, ["ident", "causT", "edgeT"], [bk])

                def nsa_B(it, i):
                    bk = "B%d" % (i % 2)
                    q0, q1, h = it["q0"], it["q1"], it["h"]
                    C.act(PT[i % 3][:, q0:q1], bank(bk)[:, q0:q1], AF.Exp, [bk, "biasN"], ["PT%d" % (i % 3)],
                          bias=biasN[:, h, it["c"] + 32:it["c"] + 33])

                def nsa_C(it, i):
                    ab = it["ab"]
                    acc = bank(ab)[:, 0:260].rearrange("p (s e) -> p s e", s=4)
                    q0, q1, h, g = it["q0"], it["q1"], it["h"], it["g"]
                    first = it["first"]
                    for sub in range(q0 // 128, q1 // 128):
                        C.mm(acc[:, sub, :], PT[i % 3][:, 128 * sub:128 * (sub + 1)], it["vap"], first,
                             (it["last"] and sub == q1 // 128 - 1), ["PT%d" % (i % 3)] + it["vres"], [ab])
                        first = False
                    if it["last"]:
                        bi = 1 if it["kind"] == "sel" else 2
                        sdd = sd[h % 2]
                        sdr = "sd%d" % (h % 2)
                        C.v("dve", "reciprocal", [ab], [sdr], out=sdd[:, 0:4], in_=acc[:, :, 64])
                        C.v("dve", "tensor_tensor", [sdr, "gates"], [sdr], out=sdd[:, 4:8], in0=sdd[:, 0:4],
                            in1=gates[:, :, 3 * h + bi], op=ALU.mult)
                        tm = tmpo[h % 2]
                        tmr = "tmpo%d" % (h % 2)
                        C.v("dve", "tensor_tensor", [ab, sdr], [tmr], out=tm[:], in0=acc[:, :, 0:64],
                            in1=bcast(sdd[:, 4:8].unsqueeze(2), [128, 4, 64]), op=ALU.mult)
                        C.v("pool", "tensor_tensor", [tmr, "oacc%d" % g], ["oacc%d" % g], out=oacc[:, :, h, :], in0=oacc[:, :, h, :],
                            in1=tm[:], op=ALU.add)

                pairs = [(sub, g) for g in range(2) for sub in range(4)]
                items = []
                abi = 0

                def hook_s1(pr):
                    return ("hook", lambda: cmp_s1(*pr))

                def hook_s3(pr):
                    return ("hook", lambda: cmp_s3(*pr))

                def add_head(h, kind):
                    nonlocal abi
                    kts = list(range(max(0, 4 * T - 4), 4 * T + 4)) if kind == "win" else list(range(0, 4 * T + 4))
                    for it in nsa_items(h, kts, kind, "C%d" % (abi % 2)):
                        items.append(("it", it))
                    abi += 1

                items.append(hook_s1(pairs[0]))
                for pi in range(4):
                    add_head(2 * pi, "win")
                    add_head(2 * pi + 1, "win")
                    items.append(hook_s3(pairs[pi]))
                    items.append(hook_s1(pairs[pi + 1]))
                for pi in range(4, 8):
                    add_head(pi - 4, "sel")
                    items.append(hook_s3(pairs[pi]))
                    if pi + 1 < 8:
                        items.append(hook_s1(pairs[pi + 1]))
                for h in range(4, 8):
                    add_head(h, "sel")
                real_idx = 0
                pending = None
                for kind_, obj in items:
                    if kind_ == "hook":
                        obj()
                        continue
                    nsa_A(obj, real_idx)
                    if pending is not None:
                        nsa_B(*pending)
                        nsa_C(*pending)
                    pending = (obj, real_idx)
                    real_idx += 1
                if pending is not None:
                    nsa_B(*pending)
                    nsa_C(*pending)
                _stop("s7")


                for g in range(2):
                    C.act(tok[:, :, 256 * g:256 * (g + 1)], oacc[:, :, 4 * g:4 * g + 4, :].rearrange("p s h e -> p s (h e)"),
                          AF.Copy, ["oacc%d" % g], ["tok0", "tok1", "tok2", "tok3"])
                for sub in range(4):
                    tbk = "A%d" % (sub % 2)
                    tpb = bank(tbk).bitcast(BF16).rearrange("p (c t) -> p c t", c=8)
                    for c in range(8):
                        C.tr(tpb[:, c, :], tok[:, sub, 128 * c:128 * (c + 1)], ident[:], ["tok%d" % sub, "ident"], [tbk])
                    evac(hT[:, :, 128 * sub:128 * (sub + 1)], tpb, [tbk], ["hT%d" % sub])
                for sub in range(2):
                    C.dma(xs[sub][:], dr["x"][row0 + 128 * sub:row0 + 128 * (sub + 1), :], [], ["xs%d" % sub])
                for sub in range(4):
                    b = sub % 2
                    xr = "xs%d" % b
                    yt = yts[b]
                    ytr = "yt%d" % b
                    pY = psD if sub % 2 == 0 else psB
                    pYn = ["D0", "D1"] if sub % 2 == 0 else ["B0", "B1"]
                    for half in range(2):
                        for c in range(8):
                            C.mm(pY[:, 512 * half:512 * (half + 1)], hT[:, c, 128 * sub:128 * (sub + 1)],
                                 woT[:, c, 512 * half:512 * (half + 1)], c == 0, c == 7, ["woT", "hT%d" % sub], [pYn[half]])
                    C.act(tok[:, sub, :], pY[:, :], AF.Square, pYn, ["tok%d" % sub, "ss"], accum=ss[:, 4 + sub:5 + sub])
                    C.act(st2[:, 4 + sub:5 + sub], ss[:, 4 + sub:5 + sub], AF.Sqrt, ["ss"], ["st2"], scale=1.0 / D, bias=EPS)
                    C.v("dve", "reciprocal", ["st2"], ["rstd"], out=rstd[:, 4 + sub:5 + sub], in_=st2[:, 4 + sub:5 + sub])
                    C.v("dve", "scalar_tensor_tensor", pYn + ["rstd", "gpost"], [ytr], out=yt[:], in0=pY[:, :],
                        scalar=rstd[:, 4 + sub:5 + sub], in1=gpost[:], op0=ALU.mult, op1=ALU.mult)
                    C.v("pool", "tensor_tensor", [ytr, xr], [xr], out=xs[b][:], in0=xs[b][:], in1=yt[:], op=ALU.add)
                    C.dma(dr["x1"][row0 + 128 * sub:row0 + 128 * (sub + 1), :], xs[b][:], [xr], [])
                    if sub + 2 < 4:
                        C.dma(xs[b][:], dr["x"][row0 + 128 * (sub + 2):row0 + 128 * (sub + 3), :], [], [xr])
        except _StopBuild:
            pass
        P.emit()


def pass_b(nc, dr, NSEQ, S):
    TB = 256
    NTB = S // TB
    with ExitStack() as st:
        def sb(name, shape, dt):
            return st.enter_context(nc.sbuf_tensor(name, shape, dt))

        def ps(name):
            return st.enter_context(nc.psum_tensor(name, [128, 1024], F32))

        P = Prog(nc)
        C = Ctx(nc, P)
        wgu = sb("wgu", [128, 8, 2 * DFF], BF16)
        wdn = sb("wdn", [128, NCH, 1024], BF16)
        wple = sb("wple", [128, 2, 1024], BF16)
        wpg = sb("wpg", [128, 8, 1024], BF16)
        gpre = sb("gpre2", [128, 1024], BF16)
        gpost = sb("gpost2", [128, 1024], F32)
        convw = sb("convw", [128, NCH, 3], F32)
        convb = sb("convb", [128, NCH], F32)
        ident = sb("identb", [128, 128], BF16)
        identf = sb("identfb", [128, 128], F32)
        x1t = sb("x1t", [128, 2, 1024], F32)
        hb = sb("hbB", [128, 2, 1024], BF16)
        hT = sb("hTB", [128, 8, TB], BF16)
        gT = sb("gT", [128, NCH, TB], BF16)
        araw = [sb("araw%d" % i, [128, TB + 2], F32) for i in range(4)]
        tt_ = [sb("tt%d" % i, [128, TB], F32) for i in range(4)]
        acarry = sb("acarry", [128, NCH, 2], F32)
        f4a = sb("f4", [128, 1024], F32)
        f4b = gT[:, 0:8, :].rearrange("p c t -> p (c t)").bitcast(F32)
        F4 = [f4a, f4b]
        F4R = [["f4"], ["gT%d" % c_ for c_ in range(8)]]
        pt = [sb("ptile%d" % i, [128, PLE], F32) for i in range(2)]
        pbf = [sb("pbf%d" % i, [128, PLE], BF16) for i in range(2)]
        pTt = [sb("pTt%d" % i, [128, 2, 128], BF16) for i in range(2)]
        ss = sb("ssB", [128, 4], F32)
        st2 = sb("st2B", [128, 4], F32)
        rstd = sb("rstdB", [128, 4], F32)
        psA, psB, psC, psD = ps("psA2"), ps("psB2"), ps("psC2"), ps("psD2")
        PSB = {"A": psA, "B": psB, "C": psC, "D": psD}

        def bank(nm):
            t = PSB[nm[0]]
            k = int(nm[1])
            return t[:, 512 * k:512 * (k + 1)]

        ev_rot = [0]

        def evac(out, in_, r, w):
            ev_rot[0] ^= 1
            if ev_rot[0]:
                C.act(out, in_, AF.Copy, r, w)
            else:
                C.v("dve", "tensor_copy", r, w, out=out, in_=in_)

        pq = "pool"
        C.v(pq, "memset", [], ["identf"], identf[:], 1.0)
        C.v(pq, "affine_select", ["identf"], ["identf"], out=identf[:], in_=identf[:], pattern=[[-1, 128]],
            compare_op=ALU.is_equal, fill=0.0, base=0, channel_multiplier=1)
        C.v("dve", "tensor_copy", ["identf"], ["ident"], out=ident[:], in_=identf[:])
        jd = sb("jdummyB", [128, 8], F32)

        def join(name, n, col):
            C.v("pool", "memset", ["%s.%d" % (name, i) for i in range(n)], [name], jd[:, col:col + 1], 0.0)

        wi = 0
        for c in range(8):
            for (a, b) in ((0, 1408), (1408, 2816), (2816, 4224), (4224, 5632)):
                C.dma(wgu[:, c, a:b], dr["w_gate_up"][128 * c:128 * (c + 1), a:b], [], ["wgu.%d" % wi], queue="pool")
                wi += 1
        join("wgu", wi, 0)
        C.dma(gpre[:], dr["mlp_pre_g"].partition_broadcast(128), [], ["gpre"], queue="pool")
        C.dma(gpost[:], dr["mlp_post_g"].partition_broadcast(128), [], ["gpost"])
        C.dma(convw[:], dr["conv_wp"], [], ["convw"])
        C.dma(convb[:], dr["conv_bp"], [], ["convb"])
        for c in range(NCH):
            C.dma(wdn[:, c, :], dr["w_down"][128 * c:128 * (c + 1), :], [], ["wdn.%d" % c], queue="pool")
        join("wdn", NCH, 1)
        for c in range(2):
            C.dma(wple[:, c, :], dr["w_ple"][128 * c:128 * (c + 1), :], [], ["wple.%d" % c], queue="pool")
        join("wple", 2, 2)
        for c in range(8):
            C.dma(wpg[:, c, :], dr["w_ple_gate"][128 * c:128 * (c + 1), :], [], ["wpg.%d" % c], queue="pool")
        join("wpg", 8, 3)

        for s in range(NSEQ):
            C.v("pool", "memset", [], ["acarry%d" % ch for ch in range(NCH)], acarry[:], 0.0)
            for T in range(NTB):
                row0 = s * S + TB * T
                for sub in range(2):
                    xr = "x1t%d" % sub
                    C.dma(x1t[:, sub, :], dr["x1"][row0 + 128 * sub:row0 + 128 * (sub + 1), :], [], [xr])
                    C.dma(pt[sub][:], dr["p"][row0 + 128 * sub:row0 + 128 * (sub + 1), :], [], ["pt%d" % sub])
                    C.act(hb[:, sub, :], x1t[:, sub, :], AF.Square, [xr], ["hb%d" % sub, "ss"], accum=ss[:, sub:sub + 1])
                    C.act(st2[:, sub:sub + 1], ss[:, sub:sub + 1], AF.Sqrt, ["ss"], ["st2"], scale=1.0 / D, bias=EPS)
                    C.v("dve", "reciprocal", ["st2"], ["rstd"], out=rstd[:, sub:sub + 1], in_=st2[:, sub:sub + 1])
                    C.v("dve", "scalar_tensor_tensor", [xr, "rstd", "gpre"], ["hb%d" % sub], out=hb[:, sub, :],
                        in0=x1t[:, sub, :], scalar=rstd[:, sub:sub + 1], in1=gpre[:], op0=ALU.mult, op1=ALU.mult)
                    tpb = bank("B0").bitcast(BF16).rearrange("p (c t) -> p c t", c=8)
                    for c in range(8):
                        C.tr(tpb[:, c, :], hb[:, sub, 128 * c:128 * (c + 1)], ident[:], ["hb%d" % sub, "ident"], ["B0"])
                    evac(hT[:, :, 128 * sub:128 * (sub + 1)], tpb, ["B0"], ["hT%d" % sub])
                hTall = ["hT0", "hT1"]
                BK7 = ("A0", "A1", "B1", "C0", "C1", "D0", "D1")

                def ffn_stage(st_, ch):
                    bk = BK7[ch % 7]
                    gp = bank(bk)[:, 0:TB]
                    up = bank(bk)[:, TB:2 * TB]
                    ar = araw[ch % 4]
                    arr = "araw%d" % (ch % 4)
                    tt = tt_[ch % 4]
                    ttr = "tt%d" % (ch % 4)
                    if st_ == 0:
                        for c in range(8):
                            C.mm(gp, wgu[:, c, 128 * ch:128 * (ch + 1)], hT[:, c, :], c == 0, False, ["wgu"] + hTall, [bk])
                        for c in range(8):
                            C.mm(up, wgu[:, c, DFF + 128 * ch:DFF + 128 * (ch + 1)], hT[:, c, :], False, c == 7, ["wgu"] + hTall, [bk])
                    elif st_ == 1:
                        C.v("pool", "tensor_copy", ["acarry%d" % ch], [arr], out=ar[:, 0:2], in_=acarry[:, ch, :])
                        C.act(ar[:, 2:TB + 2], gp, AF.Copy, [bk], [arr + "m"])
                        C.v("pool", "tensor_copy", [arr + "m"], ["acarry%d" % ch], out=acarry[:, ch, :], in_=ar[:, TB:TB + 2])
                        C.v("pool", "tensor_scalar", [arr + "m", "convw", "convb"], [ttr], out=tt[:], in0=ar[:, 2:TB + 2],
                            scalar1=convw[:, ch, 2:3], scalar2=convb[:, ch:ch + 1], op0=ALU.mult, op1=ALU.add)
                    elif st_ == 2:
                        C.v("dve", "scalar_tensor_tensor", [arr, arr + "m", ttr, "convw"], [ttr], out=tt[:], in0=ar[:, 1:TB + 1],
                            scalar=convw[:, ch, 1:2], in1=tt[:], op0=ALU.mult, op1=ALU.add)
                        C.v("dve", "scalar_tensor_tensor", [arr, arr + "m", ttr, "convw"], [ttr], out=tt[:], in0=ar[:, 0:TB],
                            scalar=convw[:, ch, 0:1], in1=tt[:], op0=ALU.mult, op1=ALU.add)
                    elif st_ == 3:
                        C.act(tt[:], tt[:], AF.Gelu_apprx_tanh, [ttr], [ttr])
                    else:
                        C.v("dve", "tensor_tensor", [bk, ttr], ["gT%d" % ch], out=gT[:, ch, :], in0=up, in1=tt[:], op=ALU.mult)

                for k in range(NCH + 4):
                    for st_ in range(5):
                        ch = k - st_
                        if 0 <= ch < NCH:
                            ffn_stage(st_, ch)
                PD = [psC, psD]
                PDn = ["C", "D"]
                for sub in range(2):
                    for half in range(2):
                        for ch in range(NCH):
                            C.mm(PD[sub][:, 512 * half:512 * (half + 1)], gT[:, ch, 128 * sub:128 * (sub + 1)],
                                 wdn[:, ch, 512 * half:512 * (half + 1)], ch == 0, ch == NCH - 1, ["gT%d" % ch, "wdn"],
                                 [PDn[sub] + str(half)])
                for sub in range(2):
                    xr = "x1t%d" % sub
                    pn = [PDn[sub] + "0", PDn[sub] + "1"]
                    C.v("pool", "tensor_copy", ["pt%d" % sub], ["pbf%d" % sub], out=pbf[sub][:], in_=pt[sub][:])
                    C.act(hb[:, sub, :], PD[sub][:, :], AF.Square, pn, ["hb%d" % sub, "ss"], accum=ss[:, 2 + sub:3 + sub])
                    C.act(st2[:, 2 + sub:3 + sub], ss[:, 2 + sub:3 + sub], AF.Sqrt, ["ss"], ["st2"], scale=1.0 / D, bias=EPS)
                    C.v("dve", "reciprocal", ["st2"], ["rstd"], out=rstd[:, 2 + sub:3 + sub], in_=st2[:, 2 + sub:3 + sub])
                    f4 = F4[sub]
                    f4r = F4R[sub]
                    C.v("dve", "scalar_tensor_tensor", pn + ["rstd", "gpost"], f4r, out=f4[:, :], in0=PD[sub][:, :],
                        scalar=rstd[:, 2 + sub:3 + sub], in1=gpost[:], op0=ALU.mult, op1=ALU.mult)
                    C.v("pool", "tensor_tensor", f4r + [xr], [xr], out=x1t[:, sub, :], in0=x1t[:, sub, :], in1=f4[:, :], op=ALU.add)
                    C.act(hb[:, sub, :], x1t[:, sub, :], AF.Copy, [xr], ["hb%d" % sub])
                for sub in range(2):
                    ptp = bank("B0").bitcast(BF16).rearrange("p (c t) -> p c t", c=8)
                    for c in range(2):
                        C.tr(ptp[:, c, :], pbf[sub][:, 128 * c:128 * (c + 1)], ident[:], ["pbf%d" % sub, "ident"], ["B0"])
                    evac(pTt[sub][:], ptp[:, 0:2, :], ["B0"], ["pTt%d" % sub])
                    tpb = bank("B0").bitcast(BF16).rearrange("p (c t) -> p c t", c=8)
                    for c in range(8):
                        C.tr(tpb[:, c, :], hb[:, sub, 128 * c:128 * (c + 1)], ident[:], ["hb%d" % sub, "ident"], ["B0"])
                    evac(hT[:, :, 128 * sub:128 * (sub + 1)], tpb, ["B0"], ["hT%d" % sub])
                for sub in range(2):
                    xr = "x1t%d" % sub
                    pn = [PDn[sub] + "0", PDn[sub] + "1"]
                    for half in range(2):
                        for c in range(2):
                            C.mm(PD[sub][:, 512 * half:512 * (half + 1)], pTt[sub][:, c, :], wple[:, c, 512 * half:512 * (half + 1)],
                                 c == 0, c == 1, ["pTt%d" % sub, "wple"], [pn[half]])
                        for c in range(8):
                            C.mm(psA[:, 512 * half:512 * (half + 1)], hT[:, c, 128 * sub:128 * (sub + 1)],
                                 wpg[:, c, 512 * half:512 * (half + 1)], c == 0, c == 7, ["hT%d" % sub, "wpg"], ["A%d" % half])
                    f4 = F4[sub]
                    f4r = F4R[sub]
                    C.act(f4[:, :], psA[:, :], AF.Sigmoid, ["A0", "A1"], f4r)
                    C.v("dve", "tensor_tensor", pn + f4r, f4r, out=f4[:, :], in0=PD[sub][:, :], in1=f4[:, :], op=ALU.mult)
                    C.v("pool", "tensor_tensor", f4r + [xr], [xr], out=x1t[:, sub, :], in0=x1t[:, sub, :], in1=f4[:, :], op=ALU.add)
                    C.dma(dr["y"][row0 + 128 * sub:row0 + 128 * (sub + 1), :], x1t[:, sub, :], [xr], [])
        P.emit()


def build_nc(NSEQ, S, debug=False, do_a=True, do_b=True):
    nc = bass.Bass("TRN2", target_bir_lowering=False)
    NTOK = NSEQ * S

    def din(name, shape):
        return nc.dram_tensor(name, shape, F32, kind="ExternalInput").ap()

    dr = {
        "x": din("x", [NTOK, D]), "p": din("p", [NTOK, PLE]),
        "attn_pre_g": din("attn_pre_g", [D]), "w_in": din("w_in", [D, WIN_W]),
        "cmp_pe_kT": din("cmp_pe_kT", [64, 32]), "cmp_w1_k": din("cmp_w1_k", [2048, 128]), "cmp_w2_k": din("cmp_w2_k", [128, 64]),
        "cmp_pe_vT": din("cmp_pe_vT", [64, 32]), "cmp_w1_v": din("cmp_w1_v", [2048, 128]), "cmp_w2_v": din("cmp_w2_v", [128, 64]),
        "sinks": din("sinks", [8]), "w_o": din("w_o", [D, D]), "attn_post_g": din("attn_post_g", [D]),
        "mlp_pre_g": din("mlp_pre_g", [D]), "w_gate_up": din("w_gate_up", [D, 2 * DFF]),
        "conv_wp": din("conv_wp", [128, NCH, 3]), "conv_bp": din("conv_bp", [128, NCH]),
        "w_down": din("w_down", [DFF, D]), "mlp_post_g": din("mlp_post_g", [D]),
        "w_ple": din("w_ple", [PLE, D]), "w_ple_gate": din("w_ple_gate", [D, D]),
    }
    dr["y"] = nc.dram_tensor("y", [NTOK, D], F32, kind="ExternalOutput").ap()
    dr["x1"] = nc.dram_tensor("x1", [NTOK, D], F32, kind="ExternalOutput" if debug else "Internal").ap()
    if do_a:
        pass_a(nc, dr, NSEQ, S)
    if do_b:
        pass_b(nc, dr, NSEQ, S)
    return nc


def prep_weights(inp):
    f = lambda a: np.ascontiguousarray(np.asarray(a, dtype=np.float32))
    perm = _win_perm()
    w = {
        "attn_pre_g": f(inp["attn_pre_g"][0]), "w_in": f(inp["w_in"][0][:, perm]),
        "cmp_pe_kT": f(inp["cmp_pe_k"][0].T), "cmp_w1_k": f(inp["cmp_w1_k"][0]), "cmp_w2_k": f(inp["cmp_w2_k"][0]),
        "cmp_pe_vT": f(inp["cmp_pe_v"][0].T), "cmp_w1_v": f(inp["cmp_w1_v"][0]), "cmp_w2_v": f(inp["cmp_w2_v"][0]),
        "sinks": f(inp["sinks"][0]), "w_o": f(inp["w_o"][0]), "attn_post_g": f(inp["attn_post_g"][0]),
        "mlp_pre_g": f(inp["mlp_pre_g"][0]), "w_gate_up": f(inp["w_gate_up"][0]),
        "conv_wp": f(np.asarray(inp["conv_w"][0]).T.reshape(NCH, 128, 3).transpose(1, 0, 2)),
        "conv_bp": f(np.asarray(inp["conv_b"][0]).reshape(NCH, 128).T),
        "w_down": f(inp["w_down"][0]), "mlp_post_g": f(inp["mlp_post_g"][0]),
        "w_ple": f(inp["w_ple"][0]), "w_ple_gate": f(inp["w_ple_gate"][0]),
    }
    return w


def kernel(**inputs):
    x = np.asarray(inputs["x"], dtype=np.float32)
    p = np.asarray(inputs["p"], dtype=np.float32)[0]
    B, S, _ = x.shape
    NSEQ = B // N_CORES
    w = prep_weights(inputs)
    nc = build_nc(NSEQ, S)
    in_maps = []
    for c in range(N_CORES):
        m = dict(w)
        m["x"] = np.ascontiguousarray(x[c * NSEQ:(c + 1) * NSEQ].reshape(NSEQ * S, D))
        m["p"] = np.ascontiguousarray(p[c * NSEQ:(c + 1) * NSEQ].reshape(NSEQ * S, PLE))
        in_maps.append(m)
    res = run_bass_kernel_spmd(nc, in_maps, core_ids=list(range(N_CORES)))
    out = np.concatenate([np.asarray(r["y"]).reshape(NSEQ, S, D) for r in res.results], axis=0)
    return out.astype(np.float32)
```

```python
import numpy as np
from contextlib import ExitStack
import concourse.bass as bass
import concourse.mybir as mybir
from concourse.bass_utils import run_bass_kernel_spmd

F32 = mybir.dt.float32
BF16 = mybir.dt.bfloat16
I32 = mybir.dt.int32
AF = mybir.ActivationFunctionType
ALU = mybir.AluOpType
AX = mybir.AxisListType

N_CORES = 8
D = 1024
HD = 64
DFF = 2816
NCH = DFF // 128
PLE = 256
EPS = 1e-6
NEGM = -30000.0
SLOPES = [2.0 ** (-8.0 * (h + 1.0) / 16.0) for h in range(16)]
SLOPE_SWA = SLOPES[:8]
SLOPE_NSA = SLOPES[8:]

QOFF = 0
RAWOFF = 1024
KSLOFF = 1280
KWNOFF = 1408
KSWOFF = 1536
TMOFF = 1664
WIN_W = 2072


def _bf16_round(v):
    a = np.array([v], dtype=np.float32).view(np.uint32)
    r = ((a + 0x7FFF + ((a >> 16) & 1)) & 0xFFFF0000).astype(np.uint32)
    return float(r.view(np.float32)[0])


def _win_perm():
    perm = []
    for h in range(8):
        perm += list(range(64 * h, 64 * h + 64))
    for h in range(8):
        perm += list(range(1304 + 64 * h, 1304 + 64 * h + 64))
    for g in range(2):
        perm += list(range(512 + 64 * g, 512 + 64 * g + 64))
        perm += list(range(640 + 64 * g, 640 + 64 * g + 64))
    for base in (768, 1024, 1816):
        perm += list(range(base, base + 128))
    perm += list(range(896, 1024)) + list(range(1152, 1280)) + list(range(1944, 2072))
    perm += list(range(1280, 1304))
    assert len(perm) == WIN_W
    return np.array(perm)


ENGS = ("pe", "act", "dve", "pool", "sp")
PSUM_BANKS = frozenset(a + b for a in "ABCD" for b in "01")
NDSEM = 8


class Prog:
    def __init__(self, nc):
        self.nc = nc
        self.ops = {e: [] for e in ENGS}
        self.res_w = {}
        self.res_r = {}
        self.dma_cnt = {"sp": 0, "pool": 0}
        self.ecount = {e: 0 for e in ENGS}
        self.dsem_cnt = {}
        self.dsem_last = {}

    def _deps(self, reads, writes):
        deps = set()
        for r in reads:
            if r in self.res_w:
                deps.add(self.res_w[r])
        for w in writes:
            if w in self.res_w:
                deps.add(self.res_w[w])
            rr = self.res_r.get(w)
            if rr:
                deps.update(rr[0].values())
                deps.update(rr[1])
        return deps

    def _commit(self, me, is_dma, reads, writes):
        for r in reads:
            rr = self.res_r.get(r)
            if rr is None:
                rr = self.res_r[r] = [{}, []]
            if is_dma:
                rr[1].append(me)
            else:
                rr[0][me[0]] = me
        for w in writes:
            self.res_w[w] = me
            self.res_r[w] = [{}, []]

    def op(self, eng, fn, reads=(), writes=()):
        pr = [r for r in reads if r in PSUM_BANKS]
        if pr:
            reads = [r for r in reads if r not in PSUM_BANKS]
            writes = list(writes) + [r for r in pr if r not in writes]
        deps = self._deps(reads, writes)
        me = (eng, len(self.ops[eng]))
        self.ecount[eng] += 1
        self.ops[eng].append((fn, deps, ("e", eng, self.ecount[eng]), False))
        self._commit(me, False, reads, writes)
        return me

    def dma(self, fn, reads=(), writes=(), queue="sp"):
        deps = self._deps(reads, writes)
        n = self.dma_cnt[queue]
        self.dma_cnt[queue] = n + 1
        j = n % NDSEM
        key = (queue, j)
        if key in self.dsem_last:
            deps.add(self.dsem_last[key])
        c = self.dsem_cnt.get(key, 0) + 1
        self.dsem_cnt[key] = c
        me = (queue, len(self.ops[queue]))
        self.ops[queue].append((fn, deps, ("d", queue, j, 16 * c), True))
        self.dsem_last[key] = me
        self._commit(me, True, reads, writes)
        return me

    def emit(self, sync_same_engine=True):
        nc = self.nc
        with ExitStack() as st:
            esem = {e: st.enter_context(nc.semaphore("s_" + e)) for e in ENGS}
            dsem = {}
            for q in ("sp", "pool"):
                for j in range(NDSEM):
                    dsem[(q, j)] = st.enter_context(nc.semaphore("d_%s%d" % (q, j)))
            block = st.enter_context(nc.Block())

            def tok_sem(tok):
                if tok[0] == "e":
                    return ("e", tok[1]), esem[tok[1]], tok[2]
                return ("d", tok[1], tok[2]), dsem[(tok[1], tok[2])], tok[3]

            def body(eng_name):
                def f(eng):
                    waited = {}
                    for (fn, deps, tok, is_dma) in self.ops[eng_name]:
                        need = {}
                        for (e2, i2) in deps:
                            t2 = self.ops[e2][i2][2]
                            if e2 == eng_name and t2[0] == "e":
                                if eng_name == "pe" or not sync_same_engine:
                                    continue
                            k, s, v = tok_sem(t2)
                            if waited.get(k, 0) >= v:
                                continue
                            if k not in need or need[k][1] < v:
                                need[k] = (s, v)
                        for k, (s, v) in need.items():
                            eng.wait_ge(s, v)
                            waited[k] = v
                        ins = fn(eng)
                        if is_dma:
                            ins.then_inc(dsem[(tok[1], tok[2])], 16)
                        else:
                            ins.then_inc(esem[eng_name], 1)
                    if eng_name in ("sp", "pool"):
                        for j in range(NDSEM):
                            c = self.dsem_cnt.get((eng_name, j), 0)
                            if c:
                                eng.wait_ge(dsem[(eng_name, j)], 16 * c)
                return f

            block.tensor(body("pe"))
            block.scalar(body("act"))
            block.vector(body("dve"))
            block.gpsimd(body("pool"))
            block.sync(body("sp"))


class Ctx:
    def __init__(self, nc, P):
        self.nc = nc
        self.P = P

    def mm(self, out, lhsT, rhs, start, stop, r, w):
        self.P.op("pe", lambda e: e.matmul(out, lhsT=lhsT, rhs=rhs, start=start, stop=stop), r, w)

    def tr(self, out, in_, ident, r, w):
        self.P.op("pe", lambda e: e.transpose(out, in_, ident), r, w)

    def act(self, out, in_, func, r, w, bias=None, scale=None, accum=None):
        kw = {}
        if bias is not None:
            kw["bias"] = bias
        if scale is not None:
            kw["scale"] = scale
        if accum is not None:
            kw["accum_out"] = accum
        self.P.op("act", lambda e: e.activation(out=out, in_=in_, func=func, **kw), r, w)

    def v(self, eng, name, r, w, *args, **kw):
        self.P.op(eng, lambda e: getattr(e, name)(*args, **kw), r, w)

    def dma(self, out, in_, r, w, queue="sp", **kw):
        self.P.dma(lambda e: e.dma_start(out=out, in_=in_, **kw), r, w, queue=queue)


def bcast(ap, shape):
    return ap.broadcast_to(shape)


import os
_STOP = os.environ.get("KSTOP", "")


class _StopBuild(Exception):
    pass


def _stop(tag):
    if _STOP == tag:
        raise _StopBuild()


def pass_a(nc, dr, NSEQ, S):
    NT = S // 512
    NKT = S // 128
    with ExitStack() as st:
        def sb(name, shape, dt):
            return st.enter_context(nc.sbuf_tensor(name, shape, dt))

        def ps(name):
            return st.enter_context(nc.psum_tensor(name, [128, 1024], F32))

        P = Prog(nc)
        C = Ctx(nc, P)

        winT = sb("winT", [128, 8, WIN_W], BF16)
        woT = sb("woT", [128, 8, 1024], BF16)
        w1kv = sb("w1kv", [128, 32, 128], BF16)
        w2kv = sb("w2kv", [128, 2, 64], BF16)
        peT = sb("peT", [128, 32], BF16)
        hbias = sb("hbias", [128, 2], F32)
        gpre = sb("gpre", [128, 1024], BF16)
        gpost = sb("gpost", [128, 1024], F32)
        ident = sb("ident", [128, 128], BF16)
        identf = sb("identf", [128, 128], F32)
        causT = sb("causT", [128, 128], BF16)
        edgeT = sb("edgeT", [128, 128], BF16)
        maskf = sb("maskf", [128, 128], F32)
        posi = sb("posi", [128, 36], I32)
        posf = sb("posf", [128, 36], F32)
        biasN = sb("biasN", [128, 8, 36], F32)
        biasS = sb("biasS", [128, 8, 2], F32)
        coli = sb("coli", [128, 1], I32)
        colf = sb("colf", [128, 1], F32)
        colb = sb("colb", [128, 8], BF16)
        dl = sb("dl", [128, 8], F32)
        sinkb = sb("sinkb", [128, 8], F32)
        sinkterm = sb("sinkterm", [128, 8], F32)
        cmpmask = sb("cmpmask", [128, 9], F32)
        Wadj = sb("Wadj", [128, 128], F32)
        qT = sb("qT", [128, 16, 512], BF16)
        kslT = sb("kslT", [128, 2, S], BF16)
        cq = sb("cq", [128, 8, 128], BF16)
        kc2 = sb("kc2", [128, 256], BF16)
        vsl = sb("vsl", [128, NKT, 2, 66], BF16)
        kwnT = sb("kwnT", [64, 2, 1024], BF16)
        vwn = sb("vwn", [128, 8, 2, 66], BF16)
        kswT = sb("kswT", [128, 2, 1024], BF16)
        vsw = sb("vsw", [128, 8, 2, 66], BF16)
        kcT = sb("kcT", [128, 2, 256], BF16)
        vcT = sb("vcT", [64, 2, 256], BF16)
        vcm = sb("vcm", [128, 2, 2, 64], BF16)
        raw = [sb("raw%d" % g, [128, 528], BF16) for g in range(2)]
        hid = sb("hid", [128, 32], BF16)
        xs = [sb("xs%d" % i, [128, 1024], F32) for i in range(2)]
        yts = [sb("yt%d" % i, [128, 1024], F32) for i in range(2)]
        tok = sb("tok", [128, 4, 1024], BF16)
        hT = sb("hT", [128, 8, 512], BF16)
        ss = sb("ss", [128, 8], F32)
        st2 = sb("st2", [128, 8], F32)
        rstd = sb("rstd", [128, 8], F32)
        gates = sb("gates", [128, 4, 24], F32)
        oacc = sb("oacc", [128, 4, 8, 64], F32)
        tmpo = [sb("tmpo%d" % i, [128, 4, 64], F32) for i in range(2)]
        tmpc = sb("tmpc", [128, 4, 64], F32)
        ef = sb("ef", [128, 4, 256], F32)
        pf = sb("pf", [128, 4, 256], F32)
        pb = sb("pb", [128, 4, 256], BF16)
        ef_flat = ef[:, :, :].rearrange("p h n -> p (h n)")
        pf_flat = pf[:, :, :].rearrange("p h n -> p (h n)")
        Indf = ef_flat[64:128, 0:512]
        rowf = ef_flat[:, 512:1024]
        rowi = pf_flat.bitcast(I32)[:, 0:512]
        pT = sb("pT", [128, 8, 128], BF16)
        psum4 = sb("psum4", [128, 264], F32)
        impA = sb("impA", [128, 64], F32)
        imp = sb("imp", [128, 64], F32)
        imp2 = sb("imp2", [128, 64], F32)
        m8 = sb("m8", [128, 16], F32)
        mb = sb("mb", [128, 128], BF16)
        mx = sb("mx", [128, 4], F32)
        negm = sb("negm", [128, 4], F32)
        sm = sb("sm", [128, 4], F32)
        rs = sb("rs", [128, 4], F32)
        sd = [sb("sd%d" % i, [128, 8], F32) for i in range(2)]
        PT = [sb("PT%d" % i, [128, 512], BF16) for i in range(3)]
        PW = [sb("PW%d" % i, [128, 4, 128], BF16) for i in range(4)]

        psA, psB, psC, psD = ps("psA"), ps("psB"), ps("psC"), ps("psD")
        PSB = {"A": psA, "B": psB, "C": psC, "D": psD}

        def bank(nm):
            t = PSB[nm[0]]
            k = int(nm[1])
            return t[:, 512 * k:512 * (k + 1)]

        pq = "pool"
        C.v(pq, "memset", [], ["identf"], identf[:], 1.0)
        C.v(pq, "affine_select", ["identf"], ["identf"], out=identf[:], in_=identf[:], pattern=[[-1, 128]],
            compare_op=ALU.is_equal, fill=0.0, base=0, channel_multiplier=1)
        C.v("dve", "tensor_copy", ["identf"], ["ident"], out=ident[:], in_=identf[:])
        C.v(pq, "memset", [], ["maskf"], maskf[:], 0.0)
        C.v(pq, "affine_select", ["maskf"], ["maskf"], out=maskf[:], in_=maskf[:], pattern=[[1, 128]],
            compare_op=ALU.is_ge, fill=NEGM, base=0, channel_multiplier=-1)
        C.v("dve", "tensor_copy", ["maskf"], ["causT"], out=causT[:], in_=maskf[:])
        C.v(pq, "memset", ["maskf"], ["maskf"], maskf[:], 0.0)
        C.v(pq, "affine_select", ["maskf"], ["maskf"], out=maskf[:], in_=maskf[:], pattern=[[-1, 128]],
            compare_op=ALU.is_ge, fill=NEGM, base=-1, channel_multiplier=1)
        C.v("dve", "tensor_copy", ["maskf"], ["edgeT"], out=edgeT[:], in_=maskf[:])
        for pc in range(S // 512):
            C.v(pq, "memset", ["ef"], ["ef"], Indf, 1.0)
            C.v(pq, "affine_select", ["ef"], ["ef"], out=Indf, in_=Indf, pattern=[[1, 512]],
                compare_op=ALU.is_ge, fill=0.0, base=512 * pc, channel_multiplier=-64)
            C.v(pq, "affine_select", ["ef"], ["ef"], out=Indf, in_=Indf, pattern=[[-1, 512]],
                compare_op=ALU.is_ge, fill=0.0, base=63 - 512 * pc, channel_multiplier=64)
            for g in range(2):
                C.v("dve", "tensor_copy", ["ef"], ["Ind"], out=kslT[64:128, g, 512 * pc:512 * (pc + 1)], in_=Indf)
        C.v(pq, "iota", [], ["posi"], posi[:], pattern=[[128, 36]], base=-32 * 128, channel_multiplier=1)
        C.v("dve", "tensor_copy", ["posi"], ["posf"], out=posf[:], in_=posi[:])
        for h in range(8):
            C.v("dve", "tensor_scalar", ["posf"], ["biasN"], out=biasN[:, h, :], in0=posf[:],
                scalar1=float(SLOPE_NSA[h]), scalar2=None, op0=ALU.mult)
            C.v("dve", "tensor_scalar", ["posf"], ["biasS"], out=biasS[:, h, :], in0=posf[:, 31:33],
                scalar1=float(SLOPE_SWA[h]), scalar2=None, op0=ALU.mult)
        C.v(pq, "memset", [], ["qTc"] + ["qT%d" % h for h in range(16)] + ["maskT%d.%d" % (g_, s_) for g_ in range(2) for s_ in range(4)], qT[:], 0.0)
        C.v(pq, "memset", [], ["cq"], cq[:], 0.0)
        C.v(pq, "memset", [], ["mb"], mb[:], 0.0)
        for h in range(8):
            hi = _bf16_round(SLOPE_NSA[h])
            lo = SLOPE_NSA[h] - hi
            C.v(pq, "memset", ["cq"], ["cq"], cq[0:1, h, :], hi)
            C.v(pq, "memset", ["cq"], ["cq"], cq[32:33, h, :], lo)
        C.v(pq, "iota", [], ["pf"], rowi[64:65, :], pattern=[[0, 4], [1, 128]], base=0, channel_multiplier=0)
        C.v("dve", "tensor_copy", ["pf"], ["ef"], out=rowf[64:65, :], in_=rowi[64:65, :])
        C.v(pq, "iota", [], ["coli"], coli[:], pattern=[[0, 1]], base=0, channel_multiplier=1)
        C.v("dve", "tensor_copy", ["coli"], ["colf"], out=colf[:], in_=coli[:])
        C.dma(sinkb[:], dr["sinks"].partition_broadcast(128), [], ["sinkb"])
        for h in range(8):
            C.v("dve", "tensor_scalar", ["ef", "qTc"], ["qTc"], out=qT[64:65, 8 + h, :], in0=rowf[64:65, :],
                scalar1=-float(SLOPE_SWA[h]), scalar2=None, op0=ALU.mult)
            C.v("dve", "tensor_scalar", ["colf"], ["colb"], out=colb[:, h:h + 1], in0=colf[:],
                scalar1=-float(SLOPE_SWA[h]), scalar2=None, op0=ALU.mult)
            C.v("dve", "scalar_tensor_tensor", ["colf", "colb"], ["dl"], out=dl[:, h:h + 1], in0=colf[:],
                scalar=float(SLOPE_SWA[h]), in1=colb[:, h:h + 1], op0=ALU.mult, op1=ALU.add)
            C.act(sinkterm[:, h:h + 1], dl[:, h:h + 1], AF.Exp, ["dl", "sinkb"], ["sinkterm"],
                  bias=sinkb[:, h:h + 1])
        C.v(pq, "memset", [], ["kcT"], kcT[:], 0.0)
        C.v(pq, "memset", [], ["kc2"], kc2[:], 0.0)
        C.v(pq, "iota", ["pf"], ["pf"], rowi[0:1, 0:256], pattern=[[16, 256]], base=0, channel_multiplier=0)
        C.v(pq, "iota", ["pf"], ["pf"], rowi[32:33, 0:256], pattern=[[16, 256]], base=0, channel_multiplier=0)
        C.v("dve", "tensor_copy", ["pf", "kc2"], ["kc2"], out=kc2[0:1, :], in_=rowi[0:1, 0:256])
        C.v("dve", "tensor_copy", ["pf", "kc2"], ["kc2"], out=kc2[32:33, :], in_=rowi[32:33, 0:256])
        C.v(pq, "memset", [], ["vcT"], vcT[:], 0.0)
        C.v(pq, "memset", [], ["cmpmask"], cmpmask[:], 0.0)
        C.v(pq, "affine_select", ["cmpmask"], ["cmpmask"], out=cmpmask[:], in_=cmpmask[:], pattern=[[-16, 9]],
            compare_op=ALU.is_ge, fill=NEGM, base=1, channel_multiplier=1)
        C.v(pq, "memset", [], ["Wadj"], Wadj[:], 0.0)
        C.v(pq, "memset", ["Wadj"], ["Wadj"], Wadj[0:64, 63:65], 1.0e6)
        C.v(pq, "memset", ["Wadj"], ["Wadj"], Wadj[0:64, 65:128], -1.0e30)
        C.v(pq, "memset", ["Wadj"], ["Wadj"], Wadj[64:128, 64:66], 1.0e6)
        C.v(pq, "memset", ["Wadj"], ["Wadj"], Wadj[64:128, 66:128], -1.0e30)
        C.v(pq, "memset", [], ["vslc"], vsl[:, :, :, 64:65], 1.0)
        C.v(pq, "memset", [], ["vwnc"], vwn[:, :, :, 64:65], 1.0)
        C.v(pq, "memset", [], ["vswc"], vsw[:, :, :, 64:65], 1.0)
        C.v(pq, "memset", [], ["kswc"], kswT[64:65, :, :], 1.0)
        for g in range(2):
            C.v(pq, "memset", [], ["raw%d" % g], raw[g][:], 0.0)

        if _STOP == "const":
            P.emit()
            return
        jd = sb("jdummy", [128, 8], F32)

        def join(name, n, col):
            C.v("pool", "memset", ["%s.%d" % (name, i) for i in range(n)], [name], jd[:, col:col + 1], 0.0)

        wi = 0
        for c in range(8):
            for (a, b) in ((0, 1036), (1036, WIN_W)):
                C.dma(winT[:, c, a:b], dr["w_in"][128 * c:128 * (c + 1), a:b], [], ["winT.%d" % wi], queue="pool")
                wi += 1
        join("winT", wi, 0)
        C.dma(gpre[:], dr["attn_pre_g"].partition_broadcast(128), [], ["gpre"], queue="pool")
        C.dma(gpost[:], dr["attn_post_g"].partition_broadcast(128), [], ["gpost"])
        C.dma(w1kv[0:64, :, :], dr["cmp_w1_k"].rearrange("(l d) h -> d l h", d=64), [], ["w1kv.0"], queue="pool")
        C.dma(w1kv[64:128, :, :], dr["cmp_w1_v"].rearrange("(l d) h -> d l h", d=64), [], ["w1kv.1"], queue="pool")
        join("w1kv", 2, 1)
        C.dma(w2kv[:, 0, :], dr["cmp_w2_k"], [], ["w2kv.0"], queue="pool")
        C.dma(w2kv[:, 1, :], dr["cmp_w2_v"], [], ["w2kv.1"], queue="pool")
        join("w2kv", 2, 2)
        C.dma(peT[0:64, :], dr["cmp_pe_kT"], [], ["peT.0"], queue="pool")
        C.dma(peT[64:128, :], dr["cmp_pe_vT"], [], ["peT.1"], queue="pool")
        join("peT", 2, 3)
        for c in range(8):
            C.dma(woT[:, c, :], dr["w_o"][128 * c:128 * (c + 1), :], [], ["woT.%d" % c], queue="pool")
        join("woT", 8, 4)
        for kv in range(2):
            rows = slice(64 * kv, 64 * kv + 64)
            for l in range(32):
                C.mm(bank("A1")[:, kv:kv + 1], w1kv[rows, l, :], peT[rows, l:l + 1], l == 0, l == 31,
                     ["w1kv", "peT"], ["A1"])
            C.v("dve", "tensor_copy", ["A1"], ["hbias"], out=hbias[:, kv:kv + 1], in_=bank("A1")[:, kv:kv + 1])

        if _STOP == "weights":
            P.emit()
            return
        sel_rot = [0]
        acc_rot = [0]
        ev_rot = [0]

        def evac(out, in_, r, w, scale=None):
            ev_rot[0] ^= 1
            if ev_rot[0]:
                if scale is None:
                    C.act(out, in_, AF.Copy, r, w)
                else:
                    C.act(out, in_, AF.Copy, r, w, scale=scale)
            else:
                if scale is None:
                    C.v("dve", "tensor_copy", r, w, out=out, in_=in_)
                else:
                    C.v("dve", "tensor_scalar", r, w, out=out, in0=in_, scalar1=scale, scalar2=None, op0=ALU.mult)

        try:
          for s in range(NSEQ):
            C.v("pool", "memset", ["psum4"], ["psum4"], psum4[:], 0.0)
            C.v("pool", "memset", ["pb"], ["pb"], pb[:], 0.0)
            for T in range(NT):
                t0 = 512 * T
                row0 = s * S + t0
                slot = T % 2
                for sub in range(4):
                    b = sub % 2
                    xr = "xs%d" % b
                    C.dma(xs[b][:], dr["x"][row0 + 128 * sub:row0 + 128 * (sub + 1), :], [], [xr])
                    C.act(tok[:, sub, :], xs[b][:], AF.Square, [xr], ["tok%d" % sub, "ss"], accum=ss[:, sub:sub + 1])
                    C.act(st2[:, sub:sub + 1], ss[:, sub:sub + 1], AF.Sqrt, ["ss"], ["st2"], scale=1.0 / D, bias=EPS)
                    C.v("dve", "reciprocal", ["st2"], ["rstd"], out=rstd[:, sub:sub + 1], in_=st2[:, sub:sub + 1])
                    C.v("dve", "scalar_tensor_tensor", [xr, "rstd", "gpre"], ["tok%d" % sub], out=tok[:, sub, :],
                        in0=xs[b][:], scalar=rstd[:, sub:sub + 1], in1=gpre[:], op0=ALU.mult, op1=ALU.mult)
                    tbk = "A%d" % (sub % 2)
                    tpb = bank(tbk).bitcast(BF16).rearrange("p (c t) -> p c t", c=8)
                    for c in range(8):
                        C.tr(tpb[:, c, :], tok[:, sub, 128 * c:128 * (c + 1)], ident[:], ["tok%d" % sub, "ident"], [tbk])
                    evac(hT[:, :, 128 * sub:128 * (sub + 1)], tpb, [tbk], ["hT%d" % sub])
                hTall = ["hT0", "hT1", "hT2", "hT3"]
                _stop("s1")

                prj = [0]

                def proj_fm(col0, M, out_ap, wres, scale=None):
                    bk = ("B0", "B1", "D0", "D1")[prj[0]]
                    prj[0] = (prj[0] + 1) % 4
                    o = bank(bk)[0:M, :]
                    for c in range(8):
                        C.mm(o, winT[:, c, col0:col0 + M], hT[:, c, :], c == 0, c == 7, ["winT"] + hTall, [bk])
                    evac(out_ap, o, [bk], wres, scale=scale)

                for h in range(16):
                    proj_fm(QOFF + 64 * h, 64, qT[0:64, h, :], ["qT%d" % h], scale=0.125)
                _stop("s2a")
                for g in range(2):
                    proj_fm(RAWOFF + 128 * g, 128, raw[g][:, 16:528], ["raw%d" % g])
                    proj_fm(KSLOFF + 64 * g, 64, kslT[0:64, g, t0:t0 + 512], ["ksl%d.%d" % (g, T)])
                    proj_fm(KWNOFF + 64 * g, 64, kwnT[:, g, 512 * slot:512 * (slot + 1)], ["kwn%d.%d" % (g, slot)])
                    proj_fm(KSWOFF + 64 * g, 64, kswT[0:64, g, 512 * slot:512 * (slot + 1)], ["ksw%d.%d" % (g, slot)])
                _stop("s2b")
                for sub in range(4):
                    bk = "C%d" % (sub % 2)
                    o = bank(bk)[:, 0:408]
                    for c in range(8):
                        C.mm(o, hT[:, c, 128 * sub:128 * (sub + 1)], winT[:, c, TMOFF:TMOFF + 408], c == 0, c == 7,
                             ["winT", "hT%d" % sub], [bk])
                    kt = 4 * T + sub
                    _stop("s2c")
                    evac(vsl[:, kt, :, 0:64], o[:, 0:128].rearrange("p (g e) -> p g e", g=2), [bk], ["vsl.%d" % T])
                    _stop("s2d")
                    evac(vwn[:, 4 * slot + sub, :, 0:64], o[:, 128:256].rearrange("p (g e) -> p g e", g=2), [bk],
                         ["vwn.%d" % slot])
                    evac(vsw[:, 4 * slot + sub, :, 0:64], o[:, 256:384].rearrange("p (g e) -> p g e", g=2), [bk],
                         ["vsw.%d" % slot])
                    _stop("s2e")
                    C.act(gates[:, sub, :], o[:, 384:408], AF.Sigmoid, [bk], ["gates"])
                    _stop("s2f")

                _stop("s2")
                if T == 0:
                    NB, n0, cbase = 31, 0, 16
                else:
                    NB, n0, cbase = 32, 32 * T - 1, 0
                for g in range(2):
                    for kv in range(2):
                        rows = slice(64 * kv, 64 * kv + 64)
                        o = bank("A1")[:, 0:NB]
                        for l in range(32):
                            C.mm(o, w1kv[rows, l, :], raw[g][rows, cbase + l:cbase + l + 16 * (NB - 1) + 1:16],
                                 l == 0, l == 31, ["w1kv", "raw%d" % g], ["A1"])
                        C.act(hid[:, 0:NB], o, AF.Gelu_apprx_tanh, ["A1", "hbias"], ["hid"], bias=hbias[:, kv:kv + 1])
                        o2 = bank("A1")[0:64, 64:64 + NB]
                        C.mm(o2, w2kv[:, kv, :], hid[:, 0:NB], True, True, ["w2kv", "hid"], ["A1"])
                        if kv == 0:
                            evac(kcT[0:64, g, n0:n0 + NB], o2, ["A1"], ["kcT"])
                        else:
                            evac(vcT[:, g, n0:n0 + NB], o2, ["A1"], ["vcT"])
                    C.v("pool", "tensor_copy", ["raw%d" % g], ["raw%d" % g], out=raw[g][:, 0:16], in_=raw[g][:, 512:528])
                ncmax = 32 * T + 31
                nnt = 2 if ncmax > 128 else 1
                vtp = bank("A1").bitcast(BF16)
                for g in range(2):
                    for nt in range(nnt):
                        C.tr(vtp[:, 0:64], vcT[:, g, 128 * nt:128 * (nt + 1)], ident[0:64, 0:64], ["vcT", "ident"], ["A1"])
                        evac(vcm[:, nt, g, :], vtp[:, 0:64], ["A1"], ["vcm"])


                def cmp_s1(sub, g):
                    a = 4 * T + sub
                    nc_ = min(8 * a + 7, 255)
                    Sc = psD[:, :].rearrange("p (h n) -> p h n", h=4)
                    pres = ["D0", "D1"]
                    C.v("pool", "memset", ["psum4"], ["psum4"], psum4[:, 1 + nc_:264], 0.0)
                    if nc_ < 256:
                        C.v("pool", "memset", ["pb"], ["pb"], pb[:, :, nc_:256], 0.0)
                    for h in range(4):
                        C.mm(Sc[:, h, 0:nc_], qT[0:64, 4 * g + h, 128 * sub:128 * (sub + 1)], kcT[0:64, g, 0:nc_],
                             True, False, ["qT%d" % (4 * g + h), "kcT"], [pres[h // 2]])
                        C.mm(Sc[:, h, 0:nc_], cq[0:33, 4 * g + h, :], kc2[0:33, 0:nc_],
                             False, True, ["cq", "kc2"], [pres[h // 2]])
                    if a == 0:
                        c0, m0 = 0, 2
                    else:
                        c0, m0 = 8 * a - 2, 0
                    wdt = nc_ - c0
                    C.v("dve", "tensor_tensor", pres + ["cmpmask"], pres, out=Sc[:, :, c0:nc_], in0=Sc[:, :, c0:nc_],
                        in1=bcast(cmpmask[:, m0:m0 + wdt].unsqueeze(1), [128, 4, wdt]), op=ALU.add)
                    C.v("dve", "tensor_reduce", pres, ["mx"], out=mx[:], in_=Sc[:, :, 0:nc_], axis=AX.X, op=ALU.max)
                    C.v("dve", "tensor_scalar", ["mx"], ["negm"], out=negm[:], in0=mx[:], scalar1=-1000.0, scalar2=-1.0,
                        op0=ALU.max, op1=ALU.mult)
                    for h in range(4):
                        C.act(ef[:, h, 0:nc_], Sc[:, h, 0:nc_], AF.Exp, [pres[h // 2], "negm"], ["ef", "sm"],
                              bias=negm[:, h:h + 1], accum=sm[:, h:h + 1])
                    C.v("dve", "tensor_scalar", ["sm"], ["rs"], out=rs[:], in0=sm[:], scalar1=1.0e-30, scalar2=None,
                        op0=ALU.max)
                    C.v("dve", "reciprocal", ["rs"], ["rs"], out=rs[:], in_=rs[:])
                    C.v("dve", "tensor_tensor", ["ef", "rs"], ["pf"], out=pf[:, :, 0:nc_], in0=ef[:, :, 0:nc_],
                        in1=bcast(rs[:, :].unsqueeze(2), [128, 4, nc_]), op=ALU.mult)
                    C.v("pool", "tensor_copy", ["pf"], ["pb"], out=pb[:, :, 0:nc_], in_=pf[:, :, 0:nc_])
                    C.v("dve", "tensor_reduce", ["pf"], ["psum4"], out=psum4[:, 1:1 + nc_],
                        in_=pf[:, :, 0:nc_].rearrange("p h n -> p n h"), axis=AX.X, op=ALU.add)
                    C.v("dve", "tensor_reduce", ["psum4"], ["impA"], out=impA[:],
                        in_=psum4[:, 0:256].rearrange("p (j f) -> p j f", f=4), axis=AX.X, op=ALU.add)
                    C.v("dve", "tensor_tensor", ["impA", "psum4"], ["imp"], out=imp[:], in0=impA[:],
                        in1=psum4[:, 4:260:4], op=ALU.add)
                    C.v("dve", "tensor_tensor", ["imp", "Wadj"], ["imp"], out=imp[:], in0=imp[:],
                        in1=Wadj[:, 64 - 2 * a:128 - 2 * a], op=ALU.add)
                    C.v("dve", "tensor_scalar", ["imp"], ["imp"], out=imp[:, 0:1], in0=imp[:, 0:1], scalar1=1.0e6,
                        scalar2=None, op0=ALU.add)
                    C.v("dve", "max", ["imp"], ["m8"], out=m8[:, 0:8], in_=imp[:])
                    C.v("dve", "match_replace", ["imp", "m8"], ["imp2"], out=imp2[:], in_to_replace=m8[:, 0:8],
                        in_values=imp[:], imm_value=-3.0e38)
                    C.v("dve", "max", ["imp2"], ["m8"], out=m8[:, 8:16], in_=imp2[:])
                    C.v("dve", "tensor_scalar", ["imp", "m8"], ["mb"], out=mb[:, 64:128], in0=imp[:], scalar1=m8[:, 15:16],
                        scalar2=NEGM, op0=ALU.is_lt, op1=ALU.mult)

                def cmp_s3(sub, g):
                    a = 4 * T + sub
                    nc_ = min(8 * a + 7, 255)
                    mtp = bank("A0").bitcast(BF16)
                    C.tr(mtp[:, 0:128], mb[:], ident[:], ["mb", "ident"], ["A0"])
                    evac(qT[64:128, 4 * g:4 * g + 4, 128 * sub:128 * (sub + 1)],
                         bcast(mtp[64:128, 0:128].unsqueeze(1), [64, 4, 128]), ["A0"], ["maskT%d.%d" % (g, sub)])
                    nn = 2 if nc_ > 128 else 1
                    ptp = bank("A1").bitcast(BF16).rearrange("p (k q) -> p k q", k=8)
                    for h in range(4):
                        for nt in range(nn):
                            C.tr(ptp[:, 2 * h + nt, :], pb[:, h, 128 * nt:128 * (nt + 1)], ident[:], ["pb", "ident"], ["A1"])
                    if nn == 2:
                        evac(pT[:], ptp, ["A1"], ["pT"])
                    else:
                        evac(pT[:, 0:8:2, :], ptp[:, 0:8:2, :], ["A1"], ["pT"])
                    oc = bank("A0")[:, 128:384].rearrange("p (h e) -> p h e", h=4)
                    first = True
                    for h in range(4):
                        for nt in range(nn):
                            C.mm(oc[:, h, :], pT[:, 2 * h + nt, :], vcm[:, nt, g, :], first, (h == 3 and nt == nn - 1),
                                 ["pT", "vcm"], ["A0"])
                            first = False
                    C.v("dve", "tensor_tensor", ["A0", "gates"], ["tmpc"], out=tmpc[:],
                        in0=oc, in1=bcast(gates[:, sub, 12 * g:12 * g + 12:3].unsqueeze(2), [128, 4, 64]), op=ALU.mult)
                    C.v("pool", "tensor_tensor", ["tmpc", "oacc%d" % g], ["oacc%d" % g], out=oacc[:, sub, 4 * g:4 * g + 4, :],
                        in0=oacc[:, sub, 4 * g:4 * g + 4, :], in1=tmpc[:], op=ALU.add)

                def swa_A(hs):
                    gs = hs // 4
                    qh = 8 + hs
                    pk = "B" if hs % 2 == 0 else "C"
                    pt_ = PSB[pk]
                    Sw = [pt_[:, 0:512].rearrange("p (s q) -> p s q", s=4), pt_[:, 512:1024].rearrange("p (s q) -> p s q", s=4)]
                    firstb = [True, True]
                    for sub in range(4):
                        a = 4 * T + sub
                        for which in range(2):
                            kt = a - 1 + which
                            if kt < 0:
                                continue
                            kslot = (kt // 4) % 2
                            kcol = 512 * kslot + 128 * (kt % 4)
                            bk = "%s%d" % (pk, which)
                            C.mm(Sw[which][:, sub, :], kswT[0:65, gs, kcol:kcol + 128], qT[0:65, qh, 128 * sub:128 * (sub + 1)],
                                 firstb[which], False, ["ksw%d.%d" % (gs, kslot), "kswc", "qT%d" % qh, "qTc"], [bk])
                            firstb[which] = False
                            C.mm(Sw[which][:, sub, :], ident[:], (edgeT if which == 0 else causT)[:], False, sub == 3,
                                 ["ident", "edgeT", "causT"], [bk])

                def swa_BC(hs):
                    gs = hs // 4
                    pk = "B" if hs % 2 == 0 else "C"
                    pt_ = PSB[pk]
                    Sw = [pt_[:, 0:512].rearrange("p (s q) -> p s q", s=4), pt_[:, 512:1024].rearrange("p (s q) -> p s q", s=4)]
                    pw = PW[2 * (hs % 2):2 * (hs % 2) + 2]
                    pwr = ["PW%d" % (2 * (hs % 2)), "PW%d" % (2 * (hs % 2) + 1)]
                    s_lo = 1 if T == 0 else 0
                    C.act(pw[0][:, s_lo:4, :], Sw[0][:, s_lo:4, :], AF.Exp, [pk + "0", "biasS"], [pwr[0]], bias=biasS[:, hs, 0:1])
                    C.act(pw[1][:], Sw[1][:], AF.Exp, [pk + "1", "biasS"], [pwr[1]], bias=biasS[:, hs, 1:2])
                    ab = "D%d" % (hs % 2)
                    acc = bank(ab)[:, 0:260].rearrange("p (s e) -> p s e", s=4)
                    first = True
                    for sub in range(4):
                        a = 4 * T + sub
                        for which in range(2):
                            kt = a - 1 + which
                            if kt < 0:
                                continue
                            kslot = (kt // 4) % 2
                            C.mm(acc[:, sub, :], pw[which][:, sub, :], vsw[:, 4 * kslot + kt % 4, gs, 0:65], first,
                                 (sub == 3 and which == 1), [pwr[which], "vsw.%d" % kslot, "vswc"], [ab])
                            first = False
                    sdd = sd[hs % 2]
                    sdr = "sd%d" % (hs % 2)
                    C.v("dve", "tensor_tensor", [ab, "sinkterm"], [sdr], out=sdd[:, 0:4], in0=acc[:, :, 64],
                        in1=bcast(sinkterm[:, hs:hs + 1], [128, 4]), op=ALU.add)
                    C.v("dve", "reciprocal", [sdr], [sdr], out=sdd[:, 4:8], in_=sdd[:, 0:4])
                    C.v("dve", "tensor_tensor", [ab, sdr], ["tok0", "tok1", "tok2", "tok3"], out=tok[:, :, 512 + 64 * hs:512 + 64 * hs + 64],
                        in0=acc[:, :, 0:64], in1=bcast(sdd[:, 4:8].unsqueeze(2), [128, 4, 64]), op=ALU.mult)

                _stop("s3")
                C.v("pool", "memset", ["oacc0", "oacc1"], ["oacc0", "oacc1"], oacc[:], 0.0)
                swa_A(0)
                for hs in range(8):
                    if hs + 1 < 8:
                        swa_A(hs + 1)
                    swa_BC(hs)
                _stop("s5")

                def nsa_items(h, kts, kind, ab):
                    g = h // 4
                    items = []
                    nk = len(kts)
                    for ki, kt in enumerate(kts):
                        c = kt - 4 * T
                        if kind == "win" and c < 0:
                            e = c + 4
                            q0, q1 = 0, 128 * (e + 1)
                            mcol, mtile = 128 * e, edgeT
                        elif c >= 0:
                            q0, q1 = 128 * c, 512
                            mcol, mtile = 128 * c, causT
                        else:
                            q0, q1 = 0, 512
                            mcol, mtile = None, None
                        if kind == "sel":
                            kap = kslT[0:128, g, 128 * kt:128 * (kt + 1)]
                            kres = "ksl%d.%d" % (g, kt // 4)
                            vap = vsl[:, kt, g, 0:65]
                            vres = ["vsl.%d" % (kt // 4), "vslc"]
                        else:
                            kslot = (kt // 4) % 2
                            kcol = 512 * kslot + 128 * (kt % 4)
                            kap = kwnT[:, g, kcol:kcol + 128]
                            kres = "kwn%d.%d" % (g, kslot)
                            vap = vwn[:, 4 * kslot + kt % 4, g, 0:65]
                            vres = ["vwn.%d" % kslot, "vwnc"]
                        items.append(dict(h=h, g=g, kind=kind, kt=kt, c=c, q0=q0, q1=q1, mcol=mcol, mtile=mtile, kap=kap,
                                          kres=kres, vap=vap, vres=vres, ab=ab, first=(ki == 0), last=(ki == nk - 1)))
                    return items

                def nsa_A(it, i):
                    bk = "B%d" % (i % 2)
                    Sb = bank(bk)
                    q0, q1, h, g = it["q0"], it["q1"], it["h"], it["g"]
                    sel = it["kind"] == "sel"
                    mcol = it["mcol"]
                    if sel:
                        C.mm(Sb[:, q0:q1], it["kap"], qT[0:128, h, q0:q1], True, mcol is None,
                             [it["kres"], "qT%d" % h, "Ind"] + ["maskT%d.%d" % (g, j) for j in range(q0 // 128, q1 // 128)], [bk])
                    else:
                        C.mm(Sb[:, q0:q1], it["kap"], qT[0:64, h, q0:q1], True, mcol is None, [it["kres"], "qT%d" % h], [bk])
                    if mcol is not None:
                        C.mm(Sb[:, mcol:mcol + 128], ident[:], it["mtile"][:], False, True, ["ident", "causT", "edgeT"], [bk])

                def nsa_B(it, i):
                    bk = "B%d" % (i % 2)
                    q0, q1, h = it["q0"], it["q1"], it["h"]
                    C.act(PT[i % 3][:, q0:q1], bank(bk)[:, q0:q1], AF.Exp, [bk, "biasN"], ["PT%d" % (i % 3)],
                          bias=biasN[:, h, it["c"] + 32:it["c"] + 33])

                def nsa_C(it, i):
                    ab = it["ab"]
                    acc = bank(ab)[:, 0:260].rearrange("p (s e) -> p s e", s=4)
                    q0, q1, h, g = it["q0"], it["q1"], it["h"], it["g"]
                    first = it["first"]
                    for sub in range(q0 // 128, q1 // 128):
                        C.mm(acc[:, sub, :], PT[i % 3][:, 128 * sub:128 * (sub + 1)], it["vap"], first,
                             (it["last"] and sub == q1 // 128 - 1), ["PT%d" % (i % 3)] + it["vres"], [ab])
                        first = False
                    if it["last"]:
                        bi = 1 if it["kind"] == "sel" else 2
                        sdd = sd[h % 2]
                        sdr = "sd%d" % (h % 2)
                        C.v("dve", "reciprocal", [ab], [sdr], out=sdd[:, 0:4], in_=acc[:, :, 64])
                        C.v("dve", "tensor_tensor", [sdr, "gates"], [sdr], out=sdd[:, 4:8], in0=sdd[:, 0:4],
                            in1=gates[:, :, 3 * h + bi], op=ALU.mult)
                        tm = tmpo[h % 2]
                        tmr = "tmpo%d" % (h % 2)
                        C.v("dve", "tensor_tensor", [ab, sdr], [tmr], out=tm[:], in0=acc[:, :, 0:64],
                            in1=bcast(sdd[:, 4:8].unsqueeze(2), [128, 4, 64]), op=ALU.mult)
                        C.v("pool", "tensor_tensor", [tmr, "oacc%d" % g], ["oacc%d" % g], out=oacc[:, :, h, :], in0=oacc[:, :, h, :],
                            in1=tm[:], op=ALU.add)

                pairs = [(sub, g) for g in range(2) for sub in range(4)]
                items = []
                abi = 0

                def hook_s1(pr):
                    return ("hook", lambda: cmp_s1(*pr))

                def hook_s3(pr):
                    return ("hook", lambda: cmp_s3(*pr))

                def add_head(h, kind):
                    nonlocal abi
                    kts = list(range(max(0, 4 * T - 4), 4 * T + 4)) if kind == "win" else list(range(0, 4 * T + 4))
                    for it in nsa_items(h, kts, kind, "C%d" % (abi % 2)):
                        items.append(("it", it))
                    abi += 1

                items.append(hook_s1(pairs[0]))
                for pi in range(4):
                    add_head(2 * pi, "win")
                    add_head(2 * pi + 1, "win")
                    items.append(hook_s3(pairs[pi]))
                    items.append(hook_s1(pairs[pi + 1]))
                for pi in range(4, 8):
                    add_head(pi - 4, "sel")
                    items.append(hook_s3(pairs[pi]))
                    if pi + 1 < 8:
                        items.append(hook_s1(pairs[pi + 1]))
                for h in range(4, 8):
                    add_head(h, "sel")
                real_idx = 0
                pending = None
                for kind_, obj in items:
                    if kind_ == "hook":
                        obj()
                        continue
                    nsa_A(obj, real_idx)
                    if pending is not None:
                        nsa_B(*pending)
                        nsa_C(*pending)
                    pending = (obj, real_idx)
                    real_idx += 1
                if pending is not None:
                    nsa_B(*pending)
                    nsa_C(*pending)
                _stop("s7")


                for g in range(2):
                    C.act(tok[:, :, 256 * g:256 * (g + 1)], oacc[:, :, 4 * g:4 * g + 4, :].rearrange("p s h e -> p s (h e)"),
                          AF.Copy, ["oacc%d" % g], ["tok0", "tok1", "tok2", "tok3"])
                for sub in range(4):
                    tbk = "A%d" % (sub % 2)
                    tpb = bank(tbk).bitcast(BF16).rearrange("p (c t) -> p c t", c=8)
                    for c in range(8):
                        C.tr(tpb[:, c, :], tok[:, sub, 128 * c:128 * (c + 1)], ident[:], ["tok%d" % sub, "ident"], [tbk])
                    evac(hT[:, :, 128 * sub:128 * (sub + 1)], tpb, [tbk], ["hT%d" % sub])
                for sub in range(2):
                    C.dma(xs[sub][:], dr["x"][row0 + 128 * sub:row0 + 128 * (sub + 1), :], [], ["xs%d" % sub])
                for sub in range(4):
                    b = sub % 2
                    xr = "xs%d" % b
                    yt = yts[b]
                    ytr = "yt%d" % b
                    pY = psD if sub % 2 == 0 else psB
                    pYn = ["D0", "D1"] if sub % 2 == 0 else ["B0", "B1"]
                    for half in range(2):
                        for c in range(8):
                            C.mm(pY[:, 512 * half:512 * (half + 1)], hT[:, c, 128 * sub:128 * (sub + 1)],
                                 woT[:, c, 512 * half:512 * (half + 1)], c == 0, c == 7, ["woT", "hT%d" % sub], [pYn[half]])
                    C.act(tok[:, sub, :], pY[:, :], AF.Square, pYn, ["tok%d" % sub, "ss"], accum=ss[:, 4 + sub:5 + sub])
                    C.act(st2[:, 4 + sub:5 + sub], ss[:, 4 + sub:5 + sub], AF.Sqrt, ["ss"], ["st2"], scale=1.0 / D, bias=EPS)
                    C.v("dve", "reciprocal", ["st2"], ["rstd"], out=rstd[:, 4 + sub:5 + sub], in_=st2[:, 4 + sub:5 + sub])
                    C.v("dve", "scalar_tensor_tensor", pYn + ["rstd", "gpost"], [ytr], out=yt[:], in0=pY[:, :],
                        scalar=rstd[:, 4 + sub:5 + sub], in1=gpost[:], op0=ALU.mult, op1=ALU.mult)
                    C.v("pool", "tensor_tensor", [ytr, xr], [xr], out=xs[b][:], in0=xs[b][:], in1=yt[:], op=ALU.add)
                    C.dma(dr["x1"][row0 + 128 * sub:row0 + 128 * (sub + 1), :], xs[b][:], [xr], [])
                    if sub + 2 < 4:
                        C.dma(xs[b][:], dr["x"][row0 + 128 * (sub + 2):row0 + 128 * (sub + 3), :], [], [xr])
        except _StopBuild:
            pass
        P.emit()


def pass_b(nc, dr, NSEQ, S):
    TB = 256
    NTB = S // TB
    with ExitStack() as st:
        def sb(name, shape, dt):
            return st.enter_context(nc.sbuf_tensor(name, shape, dt))

        def ps(name):
            return st.enter_context(nc.psum_tensor(name, [128, 1024], F32))

        P = Prog(nc)
        C = Ctx(nc, P)
        wgu = sb("wgu", [128, 8, 2 * DFF], BF16)
        wdn = sb("wdn", [128, NCH, 1024], BF16)
        wple = sb("wple", [128, 2, 1024], BF16)
        wpg = sb("wpg", [128, 8, 1024], BF16)
        gpre = sb("gpre2", [128, 1024], BF16)
        gpost = sb("gpost2", [128, 1024], F32)
        convw = sb("convw", [128, NCH, 3], F32)
        convb = sb("convb", [128, NCH], F32)
        ident = sb("identb", [128, 128], BF16)
        identf = sb("identfb", [128, 128], F32)
        x1t = sb("x1t", [128, 2, 1024], F32)
        hb = sb("hbB", [128, 2, 1024], BF16)
        hT = sb("hTB", [128, 8, TB], BF16)
        gT = sb("gT", [128, NCH, TB], BF16)
        araw = [sb("araw%d" % i, [128, TB + 2], F32) for i in range(4)]
        tt_ = [sb("tt%d" % i, [128, TB], F32) for i in range(4)]
        acarry = sb("acarry", [128, NCH, 2], F32)
        f4 = sb("f4", [128, 1024], F32)
        pt = [sb("ptile%d" % i, [128, PLE], F32) for i in range(2)]
        pbf = [sb("pbf%d" % i, [128, PLE], BF16) for i in range(2)]
        pTt = [sb("pTt%d" % i, [128, 2, 128], BF16) for i in range(2)]
        ss = sb("ssB", [128, 4], F32)
        st2 = sb("st2B", [128, 4], F32)
        rstd = sb("rstdB", [128, 4], F32)
        psA, psB, psC, psD = ps("psA2"), ps("psB2"), ps("psC2"), ps("psD2")
        PSB = {"A": psA, "B": psB, "C": psC, "D": psD}

        def bank(nm):
            t = PSB[nm[0]]
            k = int(nm[1])
            return t[:, 512 * k:512 * (k + 1)]

        ev_rot = [0]

        def evac(out, in_, r, w):
            ev_rot[0] ^= 1
            if ev_rot[0]:
                C.act(out, in_, AF.Copy, r, w)
            else:
                C.v("dve", "tensor_copy", r, w, out=out, in_=in_)

        pq = "pool"
        C.v(pq, "memset", [], ["identf"], identf[:], 1.0)
        C.v(pq, "affine_select", ["identf"], ["identf"], out=identf[:], in_=identf[:], pattern=[[-1, 128]],
            compare_op=ALU.is_equal, fill=0.0, base=0, channel_multiplier=1)
        C.v("dve", "tensor_copy", ["identf"], ["ident"], out=ident[:], in_=identf[:])
        jd = sb("jdummyB", [128, 8], F32)

        def join(name, n, col):
            C.v("pool", "memset", ["%s.%d" % (name, i) for i in range(n)], [name], jd[:, col:col + 1], 0.0)

        wi = 0
        for c in range(8):
            for (a, b) in ((0, 1408), (1408, 2816), (2816, 4224), (4224, 5632)):
                C.dma(wgu[:, c, a:b], dr["w_gate_up"][128 * c:128 * (c + 1), a:b], [], ["wgu.%d" % wi], queue="pool")
                wi += 1
        join("wgu", wi, 0)
        C.dma(gpre[:], dr["mlp_pre_g"].partition_broadcast(128), [], ["gpre"], queue="pool")
        C.dma(gpost[:], dr["mlp_post_g"].partition_broadcast(128), [], ["gpost"])
        C.dma(convw[:], dr["conv_wp"], [], ["convw"])
        C.dma(convb[:], dr["conv_bp"], [], ["convb"])
        for c in range(NCH):
            C.dma(wdn[:, c, :], dr["w_down"][128 * c:128 * (c + 1), :], [], ["wdn.%d" % c], queue="pool")
        join("wdn", NCH, 1)
        for c in range(2):
            C.dma(wple[:, c, :], dr["w_ple"][128 * c:128 * (c + 1), :], [], ["wple.%d" % c], queue="pool")
        join("wple", 2, 2)
        for c in range(8):
            C.dma(wpg[:, c, :], dr["w_ple_gate"][128 * c:128 * (c + 1), :], [], ["wpg.%d" % c], queue="pool")
        join("wpg", 8, 3)

        for s in range(NSEQ):
            C.v("pool", "memset", [], ["acarry%d" % ch for ch in range(NCH)], acarry[:], 0.0)
            for T in range(NTB):
                row0 = s * S + TB * T
                for sub in range(2):
                    xr = "x1t%d" % sub
                    C.dma(x1t[:, sub, :], dr["x1"][row0 + 128 * sub:row0 + 128 * (sub + 1), :], [], [xr])
                    C.dma(pt[sub][:], dr["p"][row0 + 128 * sub:row0 + 128 * (sub + 1), :], [], ["pt%d" % sub])
                    C.act(hb[:, sub, :], x1t[:, sub, :], AF.Square, [xr], ["hb%d" % sub, "ss"], accum=ss[:, sub:sub + 1])
                    C.act(st2[:, sub:sub + 1], ss[:, sub:sub + 1], AF.Sqrt, ["ss"], ["st2"], scale=1.0 / D, bias=EPS)
                    C.v("dve", "reciprocal", ["st2"], ["rstd"], out=rstd[:, sub:sub + 1], in_=st2[:, sub:sub + 1])
                    C.v("dve", "scalar_tensor_tensor", [xr, "rstd", "gpre"], ["hb%d" % sub], out=hb[:, sub, :],
                        in0=x1t[:, sub, :], scalar=rstd[:, sub:sub + 1], in1=gpre[:], op0=ALU.mult, op1=ALU.mult)
                    tpb = bank("B0").bitcast(BF16).rearrange("p (c t) -> p c t", c=8)
                    for c in range(8):
                        C.tr(tpb[:, c, :], hb[:, sub, 128 * c:128 * (c + 1)], ident[:], ["hb%d" % sub, "ident"], ["B0"])
                    evac(hT[:, :, 128 * sub:128 * (sub + 1)], tpb, ["B0"], ["hT%d" % sub])
                hTall = ["hT0", "hT1"]
                BK7 = ("A0", "A1", "B1", "C0", "C1", "D0", "D1")

                def ffn_stage(st_, ch):
                    bk = BK7[ch % 7]
                    gp = bank(bk)[:, 0:TB]
                    up = bank(bk)[:, TB:2 * TB]
                    ar = araw[ch % 4]
                    arr = "araw%d" % (ch % 4)
                    tt = tt_[ch % 4]
                    ttr = "tt%d" % (ch % 4)
                    if st_ == 0:
                        for c in range(8):
                            C.mm(gp, wgu[:, c, 128 * ch:128 * (ch + 1)], hT[:, c, :], c == 0, False, ["wgu"] + hTall, [bk])
                        for c in range(8):
                            C.mm(up, wgu[:, c, DFF + 128 * ch:DFF + 128 * (ch + 1)], hT[:, c, :], False, c == 7, ["wgu"] + hTall, [bk])
                    elif st_ == 1:
                        C.v("pool", "tensor_copy", ["acarry%d" % ch], [arr], out=ar[:, 0:2], in_=acarry[:, ch, :])
                        C.act(ar[:, 2:TB + 2], gp, AF.Copy, [bk], [arr + "m"])
                        C.v("pool", "tensor_copy", [arr + "m"], ["acarry%d" % ch], out=acarry[:, ch, :], in_=ar[:, TB:TB + 2])
                        C.v("pool", "tensor_scalar", [arr + "m", "convw", "convb"], [ttr], out=tt[:], in0=ar[:, 2:TB + 2],
                            scalar1=convw[:, ch, 2:3], scalar2=convb[:, ch:ch + 1], op0=ALU.mult, op1=ALU.add)
                    elif st_ == 2:
                        C.v("dve", "scalar_tensor_tensor", [arr, arr + "m", ttr, "convw"], [ttr], out=tt[:], in0=ar[:, 1:TB + 1],
                            scalar=convw[:, ch, 1:2], in1=tt[:], op0=ALU.mult, op1=ALU.add)
                        C.v("dve", "scalar_tensor_tensor", [arr, arr + "m", ttr, "convw"], [ttr], out=tt[:], in0=ar[:, 0:TB],
                            scalar=convw[:, ch, 0:1], in1=tt[:], op0=ALU.mult, op1=ALU.add)
                    elif st_ == 3:
                        C.act(tt[:], tt[:], AF.Gelu_apprx_tanh, [ttr], [ttr])
                    else:
                        C.v("dve", "tensor_tensor", [bk, ttr], ["gT%d" % ch], out=gT[:, ch, :], in0=up, in1=tt[:], op=ALU.mult)

                for k in range(NCH + 4):
                    for st_ in range(5):
                        ch = k - st_
                        if 0 <= ch < NCH:
                            ffn_stage(st_, ch)
                PD = [psC, psD]
                PDn = ["C", "D"]
                for sub in range(2):
                    for half in range(2):
                        for ch in range(NCH):
                            C.mm(PD[sub][:, 512 * half:512 * (half + 1)], gT[:, ch, 128 * sub:128 * (sub + 1)],
                                 wdn[:, ch, 512 * half:512 * (half + 1)], ch == 0, ch == NCH - 1, ["gT%d" % ch, "wdn"],
                                 [PDn[sub] + str(half)])
                for sub in range(2):
                    xr = "x1t%d" % sub
                    pn = [PDn[sub] + "0", PDn[sub] + "1"]
                    C.v("pool", "tensor_copy", ["pt%d" % sub], ["pbf%d" % sub], out=pbf[sub][:], in_=pt[sub][:])
                    C.act(hb[:, sub, :], PD[sub][:, :], AF.Square, pn, ["hb%d" % sub, "ss"], accum=ss[:, 2 + sub:3 + sub])
                    C.act(st2[:, 2 + sub:3 + sub], ss[:, 2 + sub:3 + sub], AF.Sqrt, ["ss"], ["st2"], scale=1.0 / D, bias=EPS)
                    C.v("dve", "reciprocal", ["st2"], ["rstd"], out=rstd[:, 2 + sub:3 + sub], in_=st2[:, 2 + sub:3 + sub])
                    C.v("dve", "scalar_tensor_tensor", pn + ["rstd", "gpost"], ["f4"], out=f4[:], in0=PD[sub][:, :],
                        scalar=rstd[:, 2 + sub:3 + sub], in1=gpost[:], op0=ALU.mult, op1=ALU.mult)
                    C.v("pool", "tensor_tensor", ["f4", xr], [xr], out=x1t[:, sub, :], in0=x1t[:, sub, :], in1=f4[:], op=ALU.add)
                    C.act(hb[:, sub, :], x1t[:, sub, :], AF.Copy, [xr], ["hb%d" % sub])
                for sub in range(2):
                    ptp = bank("B0").bitcast(BF16).rearrange("p (c t) -> p c t", c=8)
                    for c in range(2):
                        C.tr(ptp[:, c, :], pbf[sub][:, 128 * c:128 * (c + 1)], ident[:], ["pbf%d" % sub, "ident"], ["B0"])
                    evac(pTt[sub][:], ptp[:, 0:2, :], ["B0"], ["pTt%d" % sub])
                    tpb = bank("B0").bitcast(BF16).rearrange("p (c t) -> p c t", c=8)
                    for c in range(8):
                        C.tr(tpb[:, c, :], hb[:, sub, 128 * c:128 * (c + 1)], ident[:], ["hb%d" % sub, "ident"], ["B0"])
                    evac(hT[:, :, 128 * sub:128 * (sub + 1)], tpb, ["B0"], ["hT%d" % sub])
                for sub in range(2):
                    xr = "x1t%d" % sub
                    pn = [PDn[sub] + "0", PDn[sub] + "1"]
                    for half in range(2):
                        for c in range(2):
                            C.mm(PD[sub][:, 512 * half:512 * (half + 1)], pTt[sub][:, c, :], wple[:, c, 512 * half:512 * (half + 1)],
                                 c == 0, c == 1, ["pTt%d" % sub, "wple"], [pn[half]])
                        for c in range(8):
                            C.mm((psA if sub == 0 else psB)[:, 512 * half:512 * (half + 1)], hT[:, c, 128 * sub:128 * (sub + 1)],
                                 wpg[:, c, 512 * half:512 * (half + 1)], c == 0, c == 7, ["hT%d" % sub, "wpg"],
                                 [("A%d" if sub == 0 else "B%d") % half])
                    C.act(f4[:], (psA if sub == 0 else psB)[:, :], AF.Sigmoid, ["A0", "A1"] if sub == 0 else ["B0", "B1"], ["f4"])
                    C.v("dve", "tensor_tensor", pn + ["f4"], ["f4"], out=f4[:], in0=PD[sub][:, :], in1=f4[:], op=ALU.mult)
                    C.v("pool", "tensor_tensor", ["f4", xr], [xr], out=x1t[:, sub, :], in0=x1t[:, sub, :], in1=f4[:], op=ALU.add)
                    C.dma(dr["y"][row0 + 128 * sub:row0 + 128 * (sub + 1), :], x1t[:, sub, :], [xr], [])
        P.emit()


def build_nc(NSEQ, S, debug=False, do_a=True, do_b=True):
    nc = bass.Bass("TRN2", target_bir_lowering=False)
    NTOK = NSEQ * S

    def din(name, shape):
        return nc.dram_tensor(name, shape, F32, kind="ExternalInput").ap()

    dr = {
        "x": din("x", [NTOK, D]), "p": din("p", [NTOK, PLE]),
        "attn_pre_g": din("attn_pre_g", [D]), "w_in": din("w_in", [D, WIN_W]),
        "cmp_pe_kT": din("cmp_pe_kT", [64, 32]), "cmp_w1_k": din("cmp_w1_k", [2048, 128]), "cmp_w2_k": din("cmp_w2_k", [128, 64]),
        "cmp_pe_vT": din("cmp_pe_vT", [64, 32]), "cmp_w1_v": din("cmp_w1_v", [2048, 128]), "cmp_w2_v": din("cmp_w2_v", [128, 64]),
        "sinks": din("sinks", [8]), "w_o": din("w_o", [D, D]), "attn_post_g": din("attn_post_g", [D]),
        "mlp_pre_g": din("mlp_pre_g", [D]), "w_gate_up": din("w_gate_up", [D, 2 * DFF]),
        "conv_wp": din("conv_wp", [128, NCH, 3]), "conv_bp": din("conv_bp", [128, NCH]),
        "w_down": din("w_down", [DFF, D]), "mlp_post_g": din("mlp_post_g", [D]),
        "w_ple": din("w_ple", [PLE, D]), "w_ple_gate": din("w_ple_gate", [D, D]),
    }
    dr["y"] = nc.dram_tensor("y", [NTOK, D], F32, kind="ExternalOutput").ap()
    dr["x1"] = nc.dram_tensor("x1", [NTOK, D], F32, kind="ExternalOutput" if debug else "Internal").ap()
    if do_a:
        pass_a(nc, dr, NSEQ, S)
    if do_b:
        pass_b(nc, dr, NSEQ, S)
    return nc


def prep_weights(inp):
    f = lambda a: np.ascontiguousarray(np.asarray(a, dtype=np.float32))
    perm = _win_perm()
    w = {
        "attn_pre_g": f(inp["attn_pre_g"][0]), "w_in": f(inp["w_in"][0][:, perm]),
        "cmp_pe_kT": f(inp["cmp_pe_k"][0].T), "cmp_w1_k": f(inp["cmp_w1_k"][0]), "cmp_w2_k": f(inp["cmp_w2_k"][0]),
        "cmp_pe_vT": f(inp["cmp_pe_v"][0].T), "cmp_w1_v": f(inp["cmp_w1_v"][0]), "cmp_w2_v": f(inp["cmp_w2_v"][0]),
        "sinks": f(inp["sinks"][0]), "w_o": f(inp["w_o"][0]), "attn_post_g": f(inp["attn_post_g"][0]),
        "mlp_pre_g": f(inp["mlp_pre_g"][0]), "w_gate_up": f(inp["w_gate_up"][0]),
        "conv_wp": f(np.asarray(inp["conv_w"][0]).T.reshape(NCH, 128, 3).transpose(1, 0, 2)),
        "conv_bp": f(np.asarray(inp["conv_b"][0]).reshape(NCH, 128).T),
        "w_down": f(inp["w_down"][0]), "mlp_post_g": f(inp["mlp_post_g"][0]),
        "w_ple": f(inp["w_ple"][0]), "w_ple_gate": f(inp["w_ple_gate"][0]),
    }
    return w


def kernel(**inputs):
    x = np.asarray(inputs["x"], dtype=np.float32)
    p = np.asarray(inputs["p"], dtype=np.float32)[0]
    B, S, _ = x.shape
    NSEQ = B // N_CORES
    w = prep_weights(inputs)
    nc = build_nc(NSEQ, S)
    in_maps = []
    for c in range(N_CORES):
        m = dict(w)
        m["x"] = np.ascontiguousarray(x[c * NSEQ:(c + 1) * NSEQ].reshape(NSEQ * S, D))
        m["p"] = np.ascontiguousarray(p[c * NSEQ:(c + 1) * NSEQ].reshape(NSEQ * S, PLE))
        in_maps.append(m)
    res = run_bass_kernel_spmd(nc, in_maps, core_ids=list(range(N_CORES)))
    out = np.concatenate([np.asarray(r["y"]).reshape(NSEQ, S, D) for r in res.results], axis=0)
    return out.astype(np.float32)
```
